# Optimizing a Trainium2 kernel written in Bass

```python
import math
import jax, jax.numpy as jnp
from jax import lax
import numpy as np


D_MODEL = 1024
BATCH = 4
SEQ = 4096
DEPTH = 1
DEC_BATCH = 32
DEC_SEQ = 8
PAST_LEN = 8192
PAGE_SIZE = 128

H_RET = 8
DK_RET = 64
DV_RET = 64
D_RET = H_RET * DV_RET
RET_CHUNK = 128
H_ATT = 8
HD_ATT = 64
D_ATT = H_ATT * HD_ATT
BRANCHES = ((128, 1), (512, 4), (2048, 16))
MAX_WINDOW = 2048
Q_BLOCK = 128
D_MIX = D_RET + D_ATT
D_IN = 2 * H_RET * DK_RET + 2 * D_RET + 3 * D_ATT
D_FF = 2816
CONV_W = 3
EPS = 1e-6

kernel_name = 'hybrid_retention_dilated_attn_convffn_step'


def _rmsnorm(x, w):
    xf = x.astype(jnp.float32)
    y = xf * lax.rsqrt(jnp.mean(xf * xf, axis=-1, keepdims=True) + EPS)
    return (y * w.astype(jnp.float32)).astype(x.dtype)


def _retention(q, k, v, s0):
    B, T = q.shape[0], q.shape[1]
    C = math.gcd(T, RET_CHUNK)
    nc = T // C
    lg = jnp.log1p(-jnp.exp2(-5.0 - jnp.arange(H_RET, dtype=jnp.float32)))
    i = jnp.arange(C, dtype=jnp.float32)
    diff = i[:, None] - i[None, :]
    intra = jnp.where(diff >= 0, jnp.exp(lg[:, None, None] * jnp.maximum(diff, 0.0)), 0.0)
    q_dec = jnp.exp(lg[None, :] * (i[:, None] + 1.0))[None, :, :, None]
    k_dec = jnp.exp(lg[None, :] * (C - 1.0 - i[:, None]))[None, :, :, None]
    c_dec = jnp.exp(lg * C)[None, :, None, None]

    def to_chunks(a):
        return a.astype(jnp.float32).reshape(B, nc, C, H_RET, a.shape[-1]).swapaxes(0, 1)

    qc = to_chunks(q)
    kc = to_chunks(k) * (DK_RET ** -0.5)
    vc = to_chunks(v)

    def step(S, inp):
        qi, ki, vi = inp
        sc = jnp.einsum('bihd,bjhd->bhij', qi, ki) * intra
        o = jnp.einsum('bhij,bjhe->bihe', sc, vi) + jnp.einsum('bihd,bhde->bihe', qi, S) * q_dec
        S = S * c_dec + jnp.einsum('bjhd,bjhe->bhde', ki * k_dec, vi)
        return S, o

    S, o = lax.scan(step, s0.astype(jnp.float32), (qc, kc, vc))
    o = o.swapaxes(0, 1).reshape(B, T, H_RET, DV_RET)
    return o, S


def _dilated_attention(q, q_pos, k_all, v_all, k_start):
    slopes = jnp.exp2(-8.0 * jnp.arange(1, H_ATT + 1, dtype=jnp.float32) / H_ATT)
    qf = q.astype(jnp.float32) * (HD_ATT ** -0.5)
    ms, dens, nums = [], [], []
    for window, dil in BRANCHES:
        dist = dil * jnp.arange(window // dil + 1, dtype=jnp.int32)
        local = q_pos[:, None] - dist[None, :] - k_start
        valid = local >= 0
        idx = jnp.maximum(local, 0)
        kg = k_all[:, idx].astype(jnp.float32)
        vg = v_all[:, idx].astype(jnp.float32)
        s = jnp.einsum('bqhd,bqnhd->bhqn', qf, kg) - slopes[:, None, None] * dist.astype(jnp.float32)
        s = jnp.where(valid, s, -jnp.inf)
        m = jnp.max(s, axis=-1)
        p = jnp.exp(s - m[..., None])
        ms.append(m)
        dens.append(jnp.sum(p, axis=-1))
        nums.append(jnp.einsum('bhqn,bqnhd->bqhd', p, vg))
    M = ms[0]
    for m in ms[1:]:
        M = jnp.maximum(M, m)
    num = 0.0
    den = 0.0
    for m, dn, nm in zip(ms, dens, nums):
        sc = jnp.exp(m - M)
        num = num + nm * sc.transpose(0, 2, 1)[..., None]
        den = den + dn * sc
    return num / den.transpose(0, 2, 1)[..., None]


def _layer(x, ret_s0, win_k, win_v, conv_s0, pos0, norm1_w, w_in, ret_gn_w, ret_gn_b, w_out,
           norm2_w, w_up, conv_w, conv_b, w_down):
    B, T = x.shape[0], x.shape[1]
    h = _rmsnorm(x, norm1_w)
    proj = h @ w_in
    sizes = (H_RET * DK_RET, H_RET * DK_RET, D_RET, D_RET, D_ATT, D_ATT, D_ATT)
    points = [sum(sizes[:j]) for j in range(1, len(sizes))]
    rq, rk, rv, rg, aq, ak, av = jnp.split(proj, points, axis=-1)

    ro, ret_s = _retention(rq.reshape(B, T, H_RET, DK_RET), rk.reshape(B, T, H_RET, DK_RET),
                           rv.reshape(B, T, H_RET, DV_RET), ret_s0)
    mu = jnp.mean(ro, axis=-1, keepdims=True)
    var = jnp.mean(jnp.square(ro - mu), axis=-1, keepdims=True)
    ro = ((ro - mu) * lax.rsqrt(var + EPS)).reshape(B, T, D_RET)
    ro = ro * ret_gn_w.astype(jnp.float32) + ret_gn_b.astype(jnp.float32)
    ro = jax.nn.silu(rg.astype(jnp.float32)) * ro

    ak = ak.reshape(B, T, H_ATT, HD_ATT)
    av = av.reshape(B, T, H_ATT, HD_ATT)
    k_all = jnp.concatenate([win_k.astype(ak.dtype), ak], axis=1)
    v_all = jnp.concatenate([win_v.astype(av.dtype), av], axis=1)
    k_start = pos0 - win_k.shape[1]
    qb = math.gcd(T, Q_BLOCK)
    nb = T // qb
    q_blocks = aq.reshape(B, nb, qb, H_ATT, HD_ATT).swapaxes(0, 1)
    pos_blocks = (pos0 + jnp.arange(T, dtype=jnp.int32)).reshape(nb, qb)
    ao = lax.map(lambda blk: _dilated_attention(blk[0], blk[1], k_all, v_all, k_start),
                 (q_blocks, pos_blocks))
    ao = ao.swapaxes(0, 1).reshape(B, T, D_ATT)
    keep = min(MAX_WINDOW, k_all.shape[1])

    x = x + jnp.concatenate([ro, ao], axis=-1).astype(x.dtype) @ w_out

    h = _rmsnorm(x, norm2_w)
    ua, ub = jnp.split(h @ w_up, [D_FF], axis=-1)
    buf = jnp.concatenate([conv_s0.astype(ua.dtype), ua], axis=1)
    conv = conv_b
    for j in range(CONV_W):
        conv = conv + buf[:, j:j + T] * conv_w[j]
    x = x + (jax.nn.silu(conv) * ub) @ w_down
    return (x, ret_s.astype(ret_s0.dtype), k_all[:, -keep:], v_all[:, -keep:], buf[:, -(CONV_W - 1):])


def setup_inputs(seed: int = 0) -> dict:
    key = jax.random.key(seed)
    ks = jax.random.split(key, 20)
    win_rows = min(MAX_WINDOW, PAST_LEN)
    nrm = jax.random.normal
    f32 = jnp.float32
    return {
        'x_prompt': nrm(ks[0], (BATCH, SEQ, D_MODEL), f32),
        'x_sample': nrm(ks[1], (DEC_BATCH, DEC_SEQ, D_MODEL), f32),
        'state_ret': 0.5 * nrm(ks[2], (DEC_BATCH, H_RET, DK_RET, DV_RET), f32),
        'cache_win_k': nrm(ks[3], (DEC_BATCH, win_rows, H_ATT, HD_ATT), f32),
        'cache_win_v': nrm(ks[4], (DEC_BATCH, win_rows, H_ATT, HD_ATT), f32),
        'state_conv': nrm(ks[5], (DEC_BATCH, CONV_W - 1, D_FF), f32),
        'norm1_w': 1.0 + 0.02 * nrm(ks[6], (D_MODEL,), f32),
        'w_in': nrm(ks[7], (D_MODEL, D_IN), f32) * D_MODEL ** -0.5,
        'ret_gn_w': 1.0 + 0.02 * nrm(ks[8], (D_RET,), f32),
        'ret_gn_b': 0.02 * nrm(ks[9], (D_RET,), f32),
        'w_out': nrm(ks[10], (D_MIX, D_MODEL), f32) * D_MIX ** -0.5,
        'norm2_w': 1.0 + 0.02 * nrm(ks[11], (D_MODEL,), f32),
        'w_up': nrm(ks[12], (D_MODEL, 2 * D_FF), f32) * D_MODEL ** -0.5,
        'conv_w': nrm(ks[13], (CONV_W, D_FF), f32) * CONV_W ** -0.5,
        'conv_b': 0.02 * nrm(ks[14], (D_FF,), f32),
        'w_down': nrm(ks[15], (D_FF, D_MODEL), f32) * D_FF ** -0.5,
        'normf_w': 1.0 + 0.02 * nrm(ks[16], (D_MODEL,), f32),
    }


def reference(x_prompt, x_sample, state_ret, cache_win_k, cache_win_v, state_conv, norm1_w, w_in,
              ret_gn_w, ret_gn_b, w_out, norm2_w, w_up, conv_w, conv_b, w_down, normf_w):
    weights = (norm1_w, w_in, ret_gn_w, ret_gn_b, w_out, norm2_w, w_up, conv_w, conv_b, w_down)
    b = x_prompt.shape[0]
    zero_ret = jnp.zeros((b, H_RET, DK_RET, DV_RET), x_prompt.dtype)
    empty_kv = jnp.zeros((b, 0, H_ATT, HD_ATT), x_prompt.dtype)
    zero_conv = jnp.zeros((b, CONV_W - 1, D_FF), x_prompt.dtype)
    hp, hs = x_prompt, x_sample
    for _ in range(DEPTH):
        hp, ret_p, wk_p, wv_p, conv_p = _layer(hp, zero_ret, empty_kv, empty_kv, zero_conv, 0, *weights)
        hs, ret_s, wk_s, wv_s, conv_s = _layer(hs, state_ret, cache_win_k, cache_win_v, state_conv,
                                               PAST_LEN, *weights)
    y_prompt = _rmsnorm(hp, normf_w)
    y_sample = _rmsnorm(hs, normf_w)
    return (y_prompt, y_sample, ret_p, ret_s, wk_p, wv_p, wk_s, wv_s, conv_p, conv_s)
```

```python
import math
from contextlib import ExitStack

import numpy as np
import ml_dtypes

import concourse.bass as bass
import concourse.mybir as mybir
from concourse.bass_utils import run_bass_kernel_spmd

F32 = mybir.dt.float32
BF16 = mybir.dt.bfloat16
AF = mybir.ActivationFunctionType
ALU = mybir.AluOpType
AX = mybir.AxisListType
NPBF = ml_dtypes.bfloat16

D = 1024
H = 8
DFF = 2816
NCH = DFF // 128
EPS = 1e-6
NS = 40
G = 3
WIN = 16
NRING = WIN + G
NWS = 3
NXS = 4
NOS = 3


class Buf:
    __slots__ = ("name", "w", "r")

    def __init__(self, name):
        self.name = name
        self.w = None
        self.r = []


class Tracker:
    ENG = ("pe", "act", "dve", "pool", "sp")

    def __init__(self):
        self.streams = {e: [] for e in self.ENG}
        self.count = {e: 0 for e in self.ENG}
        self.waited = {e: {} for e in self.ENG}
        self.dma_count = []
        self.out_events = []

    def new_dma_sem(self):
        self.dma_count.append(0)
        return ("d", len(self.dma_count) - 1)

    def _deps(self, reads, writes):
        deps = []
        for b in reads:
            if b.w is not None:
                deps.append(b.w)
        for b in writes:
            if b.w is not None:
                deps.append(b.w)
            deps.extend(b.r)
        return deps

    def _emit_waits(self, eng, deps, is_dma):
        st = self.streams[eng]
        wd = self.waited[eng]
        best = {}
        for sk, val in deps:
            if sk == ("e", eng) and not is_dma:
                if eng == "pe":
                    continue
                if val <= self.count[eng] - 2:
                    continue
            if wd.get(sk, 0) >= val:
                continue
            if best.get(sk, 0) < val:
                best[sk] = val
        for sk, val in best.items():
            st.append(("wait", sk, val))
            wd[sk] = val

    def op(self, eng, emit, reads=(), writes=()):
        deps = self._deps(reads, writes)
        self._emit_waits(eng, deps, False)
        self.count[eng] += 1
        ev = (("e", eng), self.count[eng])
        self.streams[eng].append(("inst", emit, ev[0], 1))
        for b in reads:
            b.r.append(ev)
        for b in writes:
            b.w = ev
            b.r = []
        return ev

    def dma(self, eng, emit, sem, reads=(), writes=(), n=1, is_output=False):
        deps = self._deps(reads, writes)
        self._emit_waits(eng, deps, True)
        self.dma_count[sem[1]] += 16 * n
        ev = (sem, self.dma_count[sem[1]])
        self.streams[eng].append(("inst", emit, sem, 16))
        for b in reads:
            b.r.append(ev)
        for b in writes:
            b.w = ev
            b.r = []
        if is_output:
            self.out_events.append(ev)
        return ev

    def final_waits(self, eng):
        best = {}
        for sk, val in self.out_events:
            if best.get(sk, 0) < val:
                best[sk] = val
        for sk, val in best.items():
            self.streams[eng].append(("wait", sk, val))


def _mult(d):
    d = np.asarray(d)
    c = ((d >= 0) & (d <= 128)).astype(np.float32)
    c += ((d >= 0) & (d <= 512) & (d % 4 == 0)).astype(np.float32)
    c += ((d >= 0) & (d <= 2048) & (d % 16 == 0)).astype(np.float32)
    return c


def _consts(ntiles_total):
    lg = np.log1p(-np.exp2(-5.0 - np.arange(H, dtype=np.float64)))
    slopes = np.exp2(-8.0 * np.arange(1, H + 1, dtype=np.float64) / H)
    c = {}
    c["ident_bf"] = np.eye(128, dtype=np.float32).astype(NPBF)
    c["ident_f"] = np.eye(128, dtype=np.float32)
    i = np.arange(128, dtype=np.float64)
    c["qdec"] = np.exp(lg[None, :] * (i[:, None] + 1.0)).astype(np.float32)
    c["kdec"] = (np.exp(-lg[None, :] * (i[:, None] + 1.0)) * 0.125).astype(np.float32)
    c["maskT"] = (i[None, :] >= i[:, None]).astype(np.float32)
    cd = np.zeros((128, 4, 64), np.float32)
    cds = np.zeros((128, 4, 64), np.float32)
    for p in range(128):
        for pr in range(4):
            h = 2 * pr + (p // 64)
            cd[p, pr, :] = math.exp(lg[h] * 128.0)
            cds[p, pr, :] = math.exp(lg[h] * 8.0)
    c["cdec"] = cd
    c["cdec_s"] = cds
    qs = np.zeros((128, 8), np.float32)
    ks = np.zeros((128, 8), np.float32)
    valid = np.zeros(NS, bool)
    bat = np.zeros(NS, int)
    tt = np.zeros(NS, int)
    for b in range(4):
        for t in range(8):
            r = b * 10 + 2 + t
            valid[r] = True
            tt[r] = t
            qs[r] = np.exp(lg * (t + 1.0))
            ks[r] = np.exp(-lg * (t + 1.0)) * 0.125
        bat[b * 10:(b + 1) * 10] = b
    c["qdec_s"] = qs
    c["kdec_s"] = ks
    ms = np.zeros((128, 128), np.float32)
    mn = np.zeros((128, 128), np.float32)
    for j in range(NS):
        for q in range(NS):
            if valid[j] and valid[q] and bat[j] == bat[q] and tt[q] >= tt[j]:
                ms[j, q] = 1.0
                mn[j, q] = _mult(tt[q] - tt[j])
    c["maskT_s"] = ms
    c["mask_n"] = mn.astype(NPBF)
    colm = np.zeros((128, 4, NS), np.float32)
    rowm = np.zeros((128, 4), np.float32)
    for b in range(4):
        colm[:, b, b * 10 + 2:b * 10 + 10] = 1.0
        rowm[b * 10 + 2:b * 10 + 10, b] = 1.0
    c["colmask_s"] = colm.astype(NPBF)
    c["rowmask_s"] = rowm
    vfs = np.zeros((128, 1), np.float32)
    vfs[:NS, 0] = valid
    c["vflag_s"] = vfs.astype(NPBF)
    k = np.arange(128)[:, None, None]
    jj = np.arange(WIN + 1)[None, :, None]
    q = np.arange(128)[None, None, :]
    c["amask"] = _mult(q - k + 128 * jj).astype(NPBF)
    cb = np.arange(16)[None, :, None]
    t8 = np.arange(8)[None, None, :]
    c["amask_s"] = _mult(2048 + t8 - (128 * cb + k)).astype(NPBF)
    qa = np.zeros((ntiles_total, 4, H, 128), np.float32)
    ka = np.zeros((ntiles_total, 4, H, 128), np.float32)
    loc = np.arange(128, dtype=np.float64)
    for T in range(ntiles_total):
        for h in range(H):
            qa[T, 0, h] = -slopes[h] * loc
            qa[T, 1, h] = slopes[h]
            qa[T, 2, h] = slopes[h] * 128.0
            qa[T, 3, h] = -slopes[h] * 128.0 * T
            ka[T, 0, h] = 1.0
            ka[T, 1, h] = loc
            ka[T, 2, h] = T
            ka[T, 3, h] = 1.0
    c["qaug"] = qa.astype(NPBF)
    c["kaug"] = ka.astype(NPBF)
    qas = np.zeros((4, H, NS), np.float32)
    kan = np.zeros((4, H, NS), np.float32)
    for h in range(H):
        qas[0, h] = -slopes[h] * tt
        qas[1, h] = slopes[h]
        qas[2, h] = slopes[h] * 128.0
        qas[3, h] = -slopes[h] * 128.0 * 16
        kan[0, h] = 1.0
        kan[1, h] = tt
        kan[2, h] = 16
        kan[3, h] = 1.0
    c["qaug_s"] = qas.astype(NPBF)
    c["kaug_n"] = kan.astype(NPBF)
    kas = np.zeros((16, 4, H, 128), np.float32)
    for cbi in range(16):
        kas[cbi, 0] = 1.0
        kas[cbi, 1] = loc[None, :]
        kas[cbi, 2] = cbi
        kas[cbi, 3] = 1.0
    c["kaug_s"] = kas.astype(NPBF)
    kab = np.zeros((128, 128), np.float32)
    kab[64] = 1.0
    kab[65] = loc
    kab[67] = 1.0
    c["kA_base"] = kab.astype(NPBF)
    ecb = np.zeros((128, 16), np.float32)
    ecb[66] = np.arange(16)
    c["ecb"] = ecb
    return c


CONST_SHAPES = None


def build_program(NT_PRE, NT_MAIN, out_first_kv_tile, STAGE=99):
    assert NT_PRE % G == 0 and (NT_MAIN + 1) % G == 0
    NT = NT_PRE + NT_MAIN
    nc = bass.Bass("TRN2", target_bir_lowering=False)
    T = Tracker()
    es = ExitStack()

    def din(name, shape, dt=F32):
        return nc.dram_tensor(name, list(shape), dt, kind="ExternalInput").ap()

    def dout(name, shape, dt=F32):
        return nc.dram_tensor(name, list(shape), dt, kind="ExternalOutput").ap()

    x_d = din("x", [NT * 128, D])
    xs_d = din("xs", [NS, D])
    sret_d = din("sret", [4, H, 64, 64])
    ck_d = din("ck", [4, 2048, 512])
    cv_d = din("cv", [4, 2048, 512])
    sconv_d = din("sconv", [8, DFF])
    w_in_d = din("w_in", [D, 3584])
    w_out_d = din("w_out", [D, D])
    w_up_d = din("w_up", [D, 2 * DFF])
    w_down_d = din("w_down", [DFF, D])
    gnw_d = din("gnw_t", [128, 512])
    gnb_d = din("gnb_t", [128, 512])
    nwf_d = din("nwf_t", [128, D])
    nw1_d = din("nw1fm", [128, 8])
    nw2_d = din("nw2fm", [128, 8])
    convfm_d = din("convfm", [128, NCH, 4])
    vflag_d = din("vflag", [128, NT], BF16)
    cshapes = {
        "ident_bf": ([128, 128], BF16), "ident_f": ([128, 128], F32), "qdec": ([128, 8], F32),
        "kdec": ([128, 8], F32), "maskT": ([128, 128], F32), "cdec": ([128, 4, 64], F32),
        "cdec_s": ([128, 4, 64], F32), "qdec_s": ([128, 8], F32), "kdec_s": ([128, 8], F32),
        "maskT_s": ([128, 128], F32), "mask_n": ([128, 128], BF16), "colmask_s": ([128, 4, NS], BF16),
        "rowmask_s": ([128, 4], F32), "vflag_s": ([128, 1], BF16), "amask": ([128, WIN + 1, 128], BF16),
        "amask_s": ([128, 16, 8], BF16), "qaug": ([NT, 4, H, 128], BF16), "kaug": ([NT, 4, H, 128], BF16),
        "qaug_s": ([4, H, NS], BF16), "kaug_n": ([4, H, NS], BF16), "kaug_s": ([16, 4, H, 128], BF16),
        "kA_base": ([128, 128], BF16), "ecb": ([128, 16], F32),
    }
    cd = {k: din("c_" + k, shp, dt) for k, (shp, dt) in cshapes.items()}

    y_d = dout("y", [NT_MAIN * 128, D])
    ys_d = dout("ys", [NS, D])
    ret_d = dout("ret", [128, 4, 64])
    rets_d = dout("rets", [128, 4, 4, 64])
    NKV = NT - out_first_kv_tile
    wk_d = dout("wk", [NKV * 128, 512])
    wv_d = dout("wv", [NKV * 128, 512])
    wks_d = dout("wks", [4, 2048, 512])
    wvs_d = dout("wvs", [4, 2048, 512])
    convp_d = dout("convp", [2, DFF])
    convs_d = dout("convs", [8, DFF])

    win_b = nc.dram_tensor("win_b", [D, 3584], BF16).ap()
    wout_b = nc.dram_tensor("wout_b", [D, D], BF16).ap()
    wup_b = nc.dram_tensor("wup_b", [D, 2 * DFF], BF16).ap()
    wdn_b = nc.dram_tensor("wdn_b", [DFF, D], BF16).ap()

    def sb(name, shape, dt):
        return es.enter_context(nc.sbuf_tensor("s_" + name, list(shape), dt))

    def ps(name, shape, dt):
        return es.enter_context(nc.psum_tensor("p_" + name, list(shape), dt))

    Bk = [ps(f"bank{i}", [128, 512], F32) for i in range(6)]
    Bkb = [Buf(f"bank{i}") for i in range(6)]
    TR = [ps(f"trb{i}", [128, 1024], BF16) for i in range(2)]
    TRb = [Buf(f"trb{i}") for i in range(2)]
    gen_rot = [2, 3, 4, 5]
    gen_i = [0]
    tr_i = [0]

    def nextbank():
        i = gen_rot[gen_i[0] % 4]
        gen_i[0] += 1
        return Bk[i], Bkb[i]

    def nexttr():
        i = tr_i[0] % 2
        tr_i[0] += 1
        return TR[i], TRb[i]

    wring = [sb(f"wring{i}", [128, 4096], BF16) for i in range(NWS)]
    wringb = [Buf(f"wring{i}") for i in range(NWS)]
    KT = [sb(f"KT{i}", [128, H, 128], BF16) for i in range(NRING)]
    KTb = [Buf(f"KT{i}") for i in range(NRING)]
    KTab = [Buf(f"KTa{i}") for i in range(NRING)]
    VR = [sb(f"VR{i}", [128, H, 65], BF16) for i in range(NRING)]
    VRb = [Buf(f"VR{i}") for i in range(NRING)]
    xs_ = [sb(f"x{i}", [128, D], F32) for i in range(NXS)]
    xb = [Buf(f"x{i}") for i in range(NXS)]
    hT = [sb(f"hT{i}", [128, 8, 128], BF16) for i in range(NOS)]
    hTb = [Buf(f"hT{i}") for i in range(NOS)]
    NMAX = G * 128
    h2T = sb("h2T", [128, 8, NMAX], BF16)
    h2Tb = [Buf(f"h2T{g}") for g in range(G)]
    qk = [sb(f"qk{i}", [128, 1024], BF16) for i in range(NOS)]
    qkb = [Buf(f"qk{i}") for i in range(NOS)]
    qab = [Buf(f"qa{i}") for i in range(NOS)]
    kd = [sb(f"kd{i}", [128, 512], BF16) for i in range(NOS)]
    kdb = [Buf(f"kd{i}") for i in range(NOS)]
    vv = [sb(f"v{i}", [128, 512], BF16) for i in range(NOS)]
    vvb = [Buf(f"v{i}") for i in range(NOS)]
    sg = [sb(f"sg{i}", [128, 512], BF16) for i in range(NOS)]
    sgb = [Buf(f"sg{i}") for i in range(NOS)]
    gT = sb("gT", [128, NCH, NMAX], BF16)
    gTb = [Buf(f"gT{c}") for c in range(NCH)]
    gT_flat = gT[:, :, :].rearrange("p c n -> p (c n)")
    Kst = [gT_flat[:, i * 2048:(i + 1) * 2048].bitcast(F32).rearrange("p (a f) -> p a f", a=2) for i in range(2)]
    Vst = [gT_flat[:, (2 + i) * 2048:(3 + i) * 2048].bitcast(F32).rearrange("p (a f) -> p a f", a=2) for i in range(2)]
    Kstb = [Buf(f"Kst{i}") for i in range(2)]
    Vstb = [Buf(f"Vst{i}") for i in range(2)]
    S_kst = [T.new_dma_sem() for _ in range(2)]
    S_vst = [T.new_dma_sem() for _ in range(2)]
    xn = sb("xn", [128, D], BF16)
    xnb = Buf("xn")
    mix = sb("mix", [128, D], BF16)
    mixb = Buf("mix")
    tmpb16 = [sb(f"tmpb{i}", [128, 512], BF16) for i in range(G)]
    tmpb16b = [Buf(f"tmpb{i}") for i in range(G)]
    tmp_i = [0]
    Pt = [[sb(f"P{a}{b}", [128, 512], BF16) for b in range(2)] for a in range(2)]
    Ptb = [[Buf(f"P{a}{b}") for b in range(2)] for a in range(2)]
    xstg = sb("xstg", [128, D], F32)
    stage = [xstg[:, 0:512], xstg[:, 512:1024]]
    stageb = [Buf(f"stage{i}") for i in range(2)]
    stage_i = [0]
    osb = sb("osb", [128, 512], F32)
    osbb = Buf("osb")
    osq, osqb = stage[0], stageb[0]
    yb, ybb = stage[1], stageb[1]
    st = sb("st", [128, 64], F32)
    stb = Buf("st")
    uab = [sb(f"uab{i}", [128, NMAX + 2], F32) for i in range(2)]
    uabb = [Buf(f"uab{i}") for i in range(2)]
    cvt = [sb(f"cvt{i}", [128, NMAX], F32) for i in range(2)]
    cvtb = [Buf(f"cvt{i}") for i in range(2)]
    slu, slub = cvt, cvtb
    carry = sb("carry", [128, NCH, 2], F32)
    carryb = Buf("carry")
    convp_st = sb("convp_st", [128, 2, NCH], F32)
    convp_stb = Buf("convp_st")
    convs_st = sb("convs_st", [128, 8, NCH], F32)
    convs_stb = Buf("convs_st")
    sconvT = sb("sconvT", [128, 8, NCH], F32)
    sconvTb = Buf("sconvT")
    sc_tok = sb("sc_tok", [128, 2, 128], F32)
    sc_tokb = Buf("sc_tok")
    Sst = sb("Sst", [128, 4, 64], F32)
    Sstb = Buf("Sst")
    Sbf = sb("Sbf", [128, 4, 64], BF16)
    Sbfb = Buf("Sbf")
    Ss = sb("Ss", [128, 4, 4, 64], F32)
    Ssb = Buf("Ss")
    Ssbf = sb("Ssbf", [128, 4, 4, 64], BF16)
    Ssbfb = Buf("Ssbf")
    csb = {}
    csbb = Buf("consts")
    for k_ in ("ident_bf", "ident_f", "qdec", "kdec", "maskT", "cdec", "cdec_s", "qdec_s", "kdec_s",
               "maskT_s", "mask_n", "colmask_s", "rowmask_s", "vflag_s", "amask", "amask_s", "kA_base", "ecb"):
        shp, dt = cshapes[k_]
        csb[k_] = sb("k_" + k_, shp, dt)
    gnw = sb("gnw", [128, 512], F32)
    gnb = sb("gnb", [128, 512], F32)
    nwf = sb("nwf", [128, D], F32)
    nw1 = sb("nw1", [128, 8], F32)
    nw2 = sb("nw2", [128, 8], F32)
    convfm = sb("convfm", [128, NCH, 4], F32)
    vflag = sb("vflag", [128, NT], BF16)
    epsT = sb("epsT", [128, 1], F32)
    Kc = [sb(f"Kc{i}", [128, 512], BF16) for i in range(2)]
    Kcb = [Buf(f"Kc{i}") for i in range(2)]
    Vc = [sb(f"Vc{i}", [128, H, 65], BF16) for i in range(3)]
    Vcb = [Buf(f"Vc{i}") for i in range(3)]
    KTs = [sb(f"KTs{i}", [128, H, 128], BF16) for i in range(2)]
    KTsb = [Buf(f"KTs{i}") for i in range(2)]
    KTsab = [Buf(f"KTsa{i}") for i in range(2)]
    KTn = sb("KTn", [128, H, NS], BF16)
    KTnb = Buf("KTn")
    Vn = sb("Vn", [128, H, 65], BF16)
    Vnb = Buf("Vn")
    Pf = [sb(f"Pf{p}", [128, H, NS], BF16) for p in range(3)]
    Pfb = [Buf(f"Pf{p}") for p in range(3)]
    qm = sb("qm", [128, 4, 4, NS], BF16)
    qmb = Buf("qm")
    km, kmb = Kc[0], Kcb[0]
    so_tok = sb("so_tok", [128, 2, 128], F32)
    so_tokb = Buf("so_tok")

    S_const = T.new_dma_sem()
    S_wcast = [T.new_dma_sem() for _ in range(7)]
    S_stage = [T.new_dma_sem() for _ in range(2)]
    S_sotok = T.new_dma_sem()
    S_ss = T.new_dma_sem()
    S_sct = T.new_dma_sem()
    S_ktn = T.new_dma_sem()
    S_fin = T.new_dma_sem()
    S_w = [T.new_dma_sem() for _ in range(NWS)]
    S_x = [T.new_dma_sem() for _ in range(NXS)]
    S_y = [T.new_dma_sem() for _ in range(NXS)]
    S_qa = [T.new_dma_sem() for _ in range(NOS)]
    S_ka = [T.new_dma_sem() for _ in range(NRING)]
    S_out = T.new_dma_sem()
    S_kc = [T.new_dma_sem() for _ in range(2)]
    S_vc = [T.new_dma_sem() for _ in range(2)]
    S_ksa = [T.new_dma_sem() for _ in range(2)]

    def dve(fn, reads=(), writes=()):
        return T.op("dve", fn, reads, writes)

    def act(fn, reads=(), writes=()):
        return T.op("act", fn, reads, writes)

    def pool(fn, reads=(), writes=()):
        return T.op("pool", fn, reads, writes)

    def pe(fn, reads=(), writes=()):
        return T.op("pe", fn, reads, writes)

    def bc(ap, shape):
        return ap.to_broadcast(list(shape))

    wsc = {}

    const_pairs = [(csb[k_][:], cd[k_]) for k_ in csb]
    const_pairs += [(d_[:], s_) for d_, s_ in ((gnw, gnw_d), (gnb, gnb_d), (nwf, nwf_d), (nw1, nw1_d), (nw2, nw2_d),
                                                (convfm, convfm_d), (vflag, vflag_d))]
    T.dma("sp", lambda e: [e.dma_start(out=d_, in_=s_) for d_, s_ in const_pairs], S_const, writes=[csbb],
          n=len(const_pairs))

    def cast_group(gi, items):
        b_ = Buf(f"wcast{gi}")
        for key, _, _ in items:
            wsc[key] = b_
        T.dma("pool", lambda e: [e.dma_start(out=d_, in_=s_) for _, d_, s_ in items], S_wcast[gi], writes=[b_],
              n=len(items))

    for gi_, c in enumerate((1, 2, 5, 6)):
        cast_group(gi_, [(f"in{c}", win_b[:, c * 512:(c + 1) * 512], w_in_d[:, c * 512:(c + 1) * 512])])
    dve(lambda e: e.memset(epsT[:], EPS), writes=[csbb])
    dve(lambda e: e.memset(Sst[:], 0.0), writes=[Sstb])
    dve(lambda e: e.memset(Sbf[:], 0.0), writes=[Sbfb])
    dve(lambda e: e.memset(carry[:], 0.0), writes=[carryb])
    dve(lambda e: e.memset(h2T[:], 0.0), writes=h2Tb)
    for i in range(NRING):
        dve(lambda e, i=i: e.memset(KT[i][:], 0.0), writes=[KTb[i], KTab[i]])
        dve(lambda e, i=i: e.memset(VR[i][:], 0.0), writes=[VRb[i]])

    def late_setup():
        cast_group(4, [(f"in{c}", win_b[:, c * 512:(c + 1) * 512], w_in_d[:, c * 512:(c + 1) * 512]) for c in (0, 3, 4)]
                   + [(f"out{c}", wout_b[:, c * 512:(c + 1) * 512], w_out_d[:, c * 512:(c + 1) * 512]) for c in range(2)])
        items = []
        for s_ in range(6):
            c0, c1 = s_ * 512, min(DFF, s_ * 512 + 512)
            items.append((f"upA{s_}", wup_b[:, c0:c1], w_up_d[:, c0:c1]))
            items.append((f"upB{s_}", wup_b[:, DFF + c0:DFF + c1], w_up_d[:, DFF + c0:DFF + c1]))
        cast_group(5, items)
        items = []
        for s_ in range(6):
            r0, r1 = s_ * 512, min(DFF, s_ * 512 + 512)
            items.append((f"dn{s_}", wdn_b[r0:r1, :], w_down_d[r0:r1, :]))
        cast_group(6, items)

        for i in range(3):
            pool(lambda e, i=i: e.memset(Vc[i][:], 1.0), writes=[Vcb[i]])
        pool(lambda e: e.memset(Vn[:], 0.0), writes=[Vnb])
        pool(lambda e: e.memset(KTn[:], 0.0), writes=[KTnb])
        for i in range(2):
            pool(lambda e, i=i: e.memset(KTs[i][:], 0.0), writes=[KTsb[i], KTsab[i]])
        for i in range(NOS):
            pool(lambda e, i=i: e.memset(qk[i][:], 0.0), writes=[qkb[i], qab[i]])
        sret_v = sret_d.rearrange("b (pr two) d e -> two d b pr e", two=2)
        T.dma("sp", lambda e: [e.dma_start(out=Ss[par * 64:(par + 1) * 64, :, :, :], in_=sret_v[par]) for par in range(2)],
              S_ss, writes=[Ssb], n=2)
        dve(lambda e: e.tensor_copy(out=Ssbf[:], in_=Ss[:]), reads=[Ssb], writes=[Ssbfb])
        sconv_flat = sconv_d.rearrange("j (c f) -> (j c) f", f=128)
        NFL = 8 * NCH
        T.dma("sp", lambda e: [e.dma_start(out=sc_tok[:, 0, :], in_=sconv_flat[0:128, :]),
                               e.dma_start(out=sc_tok[0:NFL - 128, 1, :], in_=sconv_flat[128:NFL, :])],
              S_sct, writes=[sc_tokb], n=2)
        bk0, bk0b = nextbank()

        def f_sct(e):
            e.transpose(out=bk0[:, 0:128], in_=sc_tok[:, 0, :], identity=csb["ident_f"][:, :])
            return e.transpose(out=bk0[:, 128:NFL], in_=sc_tok[0:NFL - 128, 1, :],
                               identity=csb["ident_f"][0:NFL - 128, 0:NFL - 128])
        pe(f_sct, reads=[sc_tokb, csbb], writes=[bk0b])
        act(lambda e: e.copy(out=sconvT[:, :, :].rearrange("p j c -> p (j c)"), in_=bk0[:, 0:NFL]),
            reads=[bk0b], writes=[sconvTb])

    slab_list = []

    def slab_src(kind, idx):
        if kind == "in":
            return win_b[:, idx * 512:(idx + 1) * 512].rearrange("(k p) n -> p k n", p=128), [128, 8, 512]
        if kind == "out":
            return wout_b[:, idx * 512:(idx + 1) * 512].rearrange("(k p) n -> p k n", p=128), [128, 8, 512]
        if kind == "up":
            c0 = idx * 256
            return ([wup_b[:, c0:c0 + 256].rearrange("(k p) n -> p k n", p=128),
                     wup_b[:, DFF + c0:DFF + c0 + 256].rearrange("(k p) n -> p k n", p=128)], [128, 16, 256])
        if kind == "dn":
            r0, r1 = idx * 512, min(DFF, idx * 512 + 512)
            return wdn_b[r0:r1, :].rearrange("(c p) n -> p c n", p=128), [128, (r1 - r0) // 128, D]
        raise ValueError

    n_pre_groups = NT_PRE // G
    n_main_groups = (NT_MAIN + 1) // G
    for _ in range(n_pre_groups):
        for c in (1, 2, 5, 6):
            slab_list.append(("in", c))
    for _ in range(n_main_groups):
        for c in range(7):
            slab_list.append(("in", c))
        for c in range(2):
            slab_list.append(("out", c))
        for s_ in range(11):
            slab_list.append(("up", s_))
        for s_ in range(6):
            slab_list.append(("dn", s_))
    slab_next = [0]
    slab_cons = [0]

    def prefetch_slab():
        i = slab_next[0]
        if i >= len(slab_list):
            return
        slab_next[0] += 1
        kind, idx = slab_list[i]
        src, shp = slab_src(kind, idx)
        slot = i % NWS
        n_el = shp[1] * shp[2]
        dst = wring[slot][:, 0:n_el].rearrange("p (a b) -> p a b", a=shp[1])
        if kind == "up":
            T.dma("sp", lambda e, d=dst, s=src: [e.dma_start(out=d[:, 0:8, :], in_=s[0]),
                                                  e.dma_start(out=d[:, 8:16, :], in_=s[1])], S_w[slot],
                  reads=[wsc[f"upA{idx // 2}"], wsc[f"upB{idx // 2}"]], writes=[wringb[slot]], n=2)
        else:
            T.dma("sp", lambda e, d=dst, s=src: e.dma_start(out=d, in_=s), S_w[slot],
                  reads=[wsc[f"{kind}{idx}"]], writes=[wringb[slot]])

    def take_slab(kind, idx):
        i = slab_cons[0]
        assert slab_list[i] == (kind, idx), (slab_list[i], kind, idx)
        slab_cons[0] += 1
        slot = i % NWS
        _, shp = slab_src(kind, idx)
        n_el = shp[1] * shp[2]
        view = wring[slot][:, 0:n_el].rearrange("p (a b) -> p a b", a=shp[1])
        return view, wringb[slot]

    for _ in range(NWS):
        prefetch_slab()

    class Tile:
        pass

    def load_x(tl):
        s = tl.xslot
        if tl.kind == "S":
            src = xs_d[:, :]
        else:
            src = x_d[tl.T * 128:(tl.T + 1) * 128, :]
        T.dma("act", lambda e, s=s, src=src, nr=tl.nr: e.dma_start(out=xs_[s][:nr, :], in_=src), S_x[s],
              writes=[xb[s]])

    S_xstg = T.new_dma_sem()

    def load_x_stage(tl):
        src = xs_d[:, :] if tl.kind == "S" else x_d[tl.T * 128:(tl.T + 1) * 128, :]
        T.dma("act", lambda e, src=src, nr=tl.nr: e.dma_start(out=xstg[:nr, :], in_=src), S_xstg,
              writes=[stageb[0], stageb[1]])

    def prep_tile(tl):
        load_x_stage(tl)
        prep_norm(tl)

    def prep_norm(tl, phase="all"):
        rmsnorm_T(tl, xstg, [stageb[0], stageb[1]], nw1, lambda nr, tl=tl: tl.hbuf[:, :, :nr], tl.hbufb, phase)

    def prep_group(grp):
        for tl in grp[1]:
            prep_tile(tl)

    def rmsnorm_T(tl, xsrc, xsrcb, nwfm, dst_fn, dstb, phase="all"):
        nr = tl.nr
        if phase in ("all", "a"):
            rmsnorm_a(nr, xsrc, xsrcb)
        if phase in ("all", "b"):
            rmsnorm_b(nr, nwfm, dst_fn, dstb)

    def rmsnorm_a(nr, xsrc, xsrcb):
        act(lambda e: e.activation(out=xn[:nr, :], in_=xsrc[:nr, :], func=AF.Square, accum_out=st[:nr, 0:1]),
            reads=list(xsrcb), writes=[xnb, stb])
        act(lambda e: e.activation(out=st[:nr, 1:2], in_=st[:nr, 0:1], func=AF.Sqrt, scale=1.0 / D,
                                   bias=epsT[:nr, :]), reads=[stb, csbb], writes=[stb])
        dve(lambda e: e.reciprocal(out=st[:nr, 2:3], in_=st[:nr, 1:2]), reads=[stb], writes=[stb])
        dve(lambda e: e.tensor_scalar(out=xn[:nr, :], in0=xsrc[:nr, :], scalar1=st[:nr, 2:3], scalar2=None,
                                      op0=ALU.mult), reads=list(xsrcb) + [stb], writes=[xnb])

    def rmsnorm_b(nr, nwfm, dst_fn, dstb):
        tr, trb = nexttr()

        def f(e):
            last = None
            for k in range(8):
                last = e.transpose(out=tr[:, k * 128:k * 128 + nr], in_=xn[:nr, k * 128:(k + 1) * 128],
                                   identity=csb["ident_bf"][:nr, :nr])
            return last
        pe(f, reads=[xnb, csbb], writes=[trb])
        trv = tr[:].rearrange("p (k t) -> p k t", k=8)[:, :, :nr]
        dve(lambda e: e.tensor_tensor(out=dst_fn(nr), in0=trv, in1=bc(nwfm[:].unsqueeze(2), [128, 8, nr]),
                                      op=ALU.mult), reads=[trb, csbb], writes=[dstb])

    def proj_mm(tl, wview, wb, outbank, outbankb, lhs, lhsb, ncols=512):
        nr = tl.nr

        def f(e):
            last = None
            for k in range(8):
                last = e.matmul(outbank[:nr, :ncols], lhsT=lhs[:, k, :nr], rhs=wview[:, k, :ncols],
                                start=(k == 0), stop=(k == 7))
            return last
        pe(f, reads=[lhsb, wb], writes=[outbankb])

    def ring(Tt):
        return Tt % NRING

    def evac_slab(tl, c, bk, bkb, full):
        nr, o = tl.nr, tl.oslot
        S_tile = tl.kind == "S"
        if c == 0:
            ti = tl.g
            qd_ = csb["qdec_s"] if S_tile else csb["qdec"]
            dve(lambda e: e.tensor_tensor(out=tmpb16[ti][:nr, :].rearrange("p (h d) -> p h d", h=8),
                                          in0=bk[:nr, :].rearrange("p (h d) -> p h d", h=8),
                                          in1=bc(qd_[:nr, :].unsqueeze(2), [nr, 8, 64]), op=ALU.mult),
                reads=[bkb, csbb], writes=[tmpb16b[ti]])
            tl.post.append(("tr4", ti, 0))
        elif c == 1:
            kd_ = csb["kdec_s"] if S_tile else csb["kdec"]
            dve(lambda e: e.tensor_tensor(out=kd[o][:nr, :].rearrange("p (h d) -> p h d", h=8),
                                          in0=bk[:nr, :].rearrange("p (h d) -> p h d", h=8),
                                          in1=bc(kd_[:nr, :].unsqueeze(2), [nr, 8, 64]), op=ALU.mult),
                reads=[bkb, csbb], writes=[kdb[o]])
            if full:
                tl.post.append(("tr4k", None, 512))
        elif c == 2:
            act(lambda e: e.copy(out=vv[o][:nr, :], in_=bk[:nr, :]), reads=[bkb], writes=[vvb[o]])
        elif c == 3:
            act(lambda e: e.activation(out=sg[o][:nr, :], in_=bk[:nr, :], func=AF.Silu), reads=[bkb],
                writes=[sgb[o]])
        elif c == 4:
            ti = tl.g
            act(lambda e: e.activation(out=tmpb16[ti][:nr, :], in_=bk[:nr, :], func=AF.Copy, scale=0.125),
                reads=[bkb], writes=[tmpb16b[ti]])
            tl.post.append(("tr8q", ti, None))
        elif c == 5:
            si = stage_i[0] % 2
            stage_i[0] += 1
            act(lambda e: e.copy(out=stage[si][:nr, :], in_=bk[:nr, :]), reads=[bkb], writes=[stageb[si]])
            out_kv(tl, si, wk_d, wks_d)
            ti = tl.g
            dve(lambda e: e.tensor_copy(out=tmpb16[ti][:nr, :], in_=stage[si][:nr, :]), reads=[stageb[si]],
                writes=[tmpb16b[ti]])
            tl.post.append(("tr8k", ti, None))
        elif c == 6:
            si = stage_i[0] % 2
            stage_i[0] += 1
            act(lambda e: e.copy(out=stage[si][:nr, :], in_=bk[:nr, :]), reads=[bkb], writes=[stageb[si]])
            out_kv(tl, si, wv_d, wvs_d)
            if S_tile:
                dve(lambda e: e.tensor_copy(out=Vn[:nr, :, 0:64],
                                            in_=stage[si][:nr, :].rearrange("p (h d) -> p h d", h=8)),
                    reads=[stageb[si]], writes=[Vnb])
                dve(lambda e: e.tensor_copy(out=Vn[:nr, :, 64:65],
                                            in_=bc(csb["vflag_s"][:nr, 0:1].unsqueeze(2), [nr, 8, 1])),
                    reads=[csbb], writes=[Vnb])
            else:
                rs = ring(tl.T)
                dve(lambda e: e.tensor_copy(out=VR[rs][:nr, :, 0:64],
                                            in_=stage[si][:nr, :].rearrange("p (h d) -> p h d", h=8)),
                    reads=[stageb[si]], writes=[VRb[rs]])
                dve(lambda e: e.tensor_copy(out=VR[rs][:nr, :, 64:65],
                                            in_=bc(vflag[:nr, tl.T:tl.T + 1].unsqueeze(2), [nr, 8, 1])),
                    reads=[csbb], writes=[VRb[rs]])

    def out_kv(tl, si, dstP, dstS):
        nr = tl.nr
        if tl.kind == "S":
            def f(e):
                r = []
                for b in range(4):
                    r.append(e.dma_start(out=dstS[b, 2040:2048, :], in_=stage[si][b * 10 + 2:b * 10 + 10, :]))
                return r
            T.dma("pool", f, S_stage[si], reads=[stageb[si]], n=4, is_output=True)
        elif tl.T >= out_first_kv_tile:
            r0 = (tl.T - out_first_kv_tile) * 128
            T.dma("pool", lambda e: e.dma_start(out=dstP[r0:r0 + 128, :], in_=stage[si][:nr, :]), S_stage[si],
                  reads=[stageb[si]], is_output=True)

    def run_post(tl):
        nr, o = tl.nr, tl.oslot
        for kind, ti, _ in tl.post:
            if kind in ("tr4", "tr4k"):
                src = tmpb16[ti] if kind == "tr4" else kd[o]
                srcb = tmpb16b[ti] if kind == "tr4" else kdb[o]
                off = 0 if kind == "tr4" else 512
                tr, trb = nexttr()

                def f(e, src=src, tr=tr):
                    last = None
                    for pr in range(4):
                        last = e.transpose(out=tr[:, pr * 128:pr * 128 + nr], in_=src[:nr, pr * 128:(pr + 1) * 128],
                                           identity=csb["ident_bf"][:nr, :nr])
                    return last
                pe(f, reads=[srcb, csbb], writes=[trb])
                act(lambda e, tr=tr, off=off: e.copy(
                    out=qk[o][:, off:off + 512].rearrange("p (a t) -> p a t", a=4)[:, :, :nr],
                    in_=tr[:, 0:512].rearrange("p (a t) -> p a t", a=4)[:, :, :nr]),
                    reads=[trb], writes=[qkb[o]])
            elif kind in ("tr8q", "tr8k"):
                tr, trb = nexttr()

                def f(e, ti=ti, tr=tr):
                    last = None
                    for h in range(8):
                        last = e.transpose(out=tr[0:64, h * 128:h * 128 + nr], in_=tmpb16[ti][:nr, h * 64:(h + 1) * 64],
                                           identity=csb["ident_bf"][:nr, :nr])
                    return last
                pe(f, reads=[tmpb16b[ti], csbb], writes=[trb])
                trv = tr[0:64, :].rearrange("p (h t) -> p h t", h=8)[:, :, :nr]
                if kind == "tr8q":
                    QTv = qk[o][0:68, :].rearrange("p (h t) -> p h t", h=8)
                    dve(lambda e, trv=trv, QTv=QTv: e.tensor_copy(out=QTv[0:64, :, :nr], in_=trv),
                        reads=[trb], writes=[qkb[o]])
                    if tl.kind == "S":
                        T.dma("pool", lambda e, QTv=QTv: e.dma_start(out=QTv[64:68, :, :NS], in_=cd["qaug_s"]),
                              S_qa[o], reads=[qkb[o]], writes=[qab[o]])
                    else:
                        T.dma("pool", lambda e, QTv=QTv: e.dma_start(out=QTv[64:68, :, :], in_=cd["qaug"][tl.T]),
                              S_qa[o], reads=[qkb[o]], writes=[qab[o]])
                else:
                    if tl.kind == "S":
                        act(lambda e, trv=trv: e.copy(out=KTn[0:64, :, :nr], in_=trv), reads=[trb], writes=[KTnb])
                        T.dma("pool", lambda e: e.dma_start(out=KTn[64:68, :, :], in_=cd["kaug_n"]), S_ktn,
                              writes=[KTnb])
                    else:
                        rs = ring(tl.T)
                        act(lambda e, trv=trv, rs=rs: e.copy(out=KT[rs][0:64, :, :nr], in_=trv), reads=[trb],
                            writes=[KTb[rs]])
                        T.dma("pool", lambda e, rs=rs: e.dma_start(out=KT[rs][64:68, :, :], in_=cd["kaug"][tl.T]),
                              S_ka[rs], reads=[KTb[rs]], writes=[KTab[rs]])
        tl.post = []

    def retention(tl):
        nr, o = tl.nr, tl.oslot
        S_tile = tl.kind == "S"
        qdT = qk[o][:, 0:512].rearrange("p (a t) -> p a t", a=4)
        kdT = qk[o][:, 512:1024].rearrange("p (a t) -> p a t", a=4)
        mk = csb["maskT_s"] if S_tile else csb["maskT"]

        def f(e):
            last = None
            for h in range(8):
                hp, pr = h % 2, h // 2
                last = e.matmul(Bk[hp][:nr, pr * 128:pr * 128 + nr],
                                lhsT=kdT[hp * 64:(hp + 1) * 64, pr, :nr], rhs=qdT[hp * 64:(hp + 1) * 64, pr, :nr],
                                start=True, stop=True)
            return last
        pe(f, reads=[qkb[o]], writes=[Bkb[0], Bkb[1]])
        for hb in range(2):
            if S_tile:
                dve(lambda e, hb=hb: e.memset(Pt[1][hb][:, :], 0.0), writes=[Ptb[1][hb]])
            dve(lambda e, hb=hb: e.tensor_tensor(
                out=Pt[1][hb][:nr, :].rearrange("p (a t) -> p a t", a=4)[:, :, :nr],
                in0=Bk[hb][:nr, :].rearrange("p (a t) -> p a t", a=4)[:, :, :nr],
                in1=bc(mk[:nr, :nr].unsqueeze(1), [nr, 4, nr]), op=ALU.mult),
                reads=[Bkb[hb], csbb], writes=[Ptb[1][hb]])
        if STAGE < 2.31:
            return
        if S_tile:
            for b in range(4):
                dve(lambda e, b=b: e.tensor_tensor(out=qm[:, b, :, :], in0=qdT[:, :, :NS],
                                                   in1=bc(csb["colmask_s"][:, b, :].unsqueeze(1), [128, 4, NS]),
                                                   op=ALU.mult), reads=[qkb[o], csbb], writes=[qmb])
        bo2 = [nextbank(), nextbank()]

        def f2(e):
            last = None
            for hp in range(2):
                bo = bo2[hp][0]
                for pr in range(4):
                    h = 2 * pr + hp
                    e.matmul(bo[:nr, pr * 64:(pr + 1) * 64], lhsT=Pt[1][hp][:, pr * 128:pr * 128 + nr],
                             rhs=vv[o][:, h * 64:(h + 1) * 64], start=True, stop=False, skip_group_check=True)
                    if S_tile:
                        for b in range(4):
                            last = e.matmul(bo[:nr, pr * 64:(pr + 1) * 64], lhsT=qm[hp * 64:(hp + 1) * 64, b, pr, :nr],
                                            rhs=Ssbf[hp * 64:(hp + 1) * 64, b, pr, :], start=False, stop=(b == 3),
                                            skip_group_check=True)
                    else:
                        last = e.matmul(bo[:nr, pr * 64:(pr + 1) * 64], lhsT=qdT[hp * 64:(hp + 1) * 64, pr, :nr],
                                        rhs=Sbf[hp * 64:(hp + 1) * 64, pr, :], start=False, stop=True,
                                        skip_group_check=True)
            return last
        pe(f2, reads=[Ptb[1][0], Ptb[1][1], vvb[o], qkb[o], qmb, Sbfb, Ssbfb], writes=[bo2[0][1], bo2[1][1]])
        if STAGE < 2.32:
            return
        for hp in range(2):
            act(lambda e, hp=hp: e.copy(out=osb[:nr, :].rearrange("p (a w d) -> p a w d", a=4, w=2)[:, :, hp, :],
                                        in_=bo2[hp][0][:nr, 0:256].rearrange("p (a d) -> p a d", a=4)),
                reads=[bo2[hp][1]], writes=[osbb])
        act(lambda e: e.activation(out=osq[:nr, :], in_=osb[:nr, :], func=AF.Square), reads=[osbb], writes=[osqb])
        dve(lambda e: e.tensor_reduce(out=st[:nr, 8:16], in_=osb[:nr, :].rearrange("p (h d) -> p h d", h=8),
                                      axis=AX.X, op=ALU.add), reads=[osbb], writes=[stb])
        dve(lambda e: e.tensor_reduce(out=st[:nr, 16:24], in_=osq[:nr, :].rearrange("p (h d) -> p h d", h=8),
                                      axis=AX.X, op=ALU.add), reads=[osqb], writes=[stb])
        dve(lambda e: e.tensor_scalar(out=st[:nr, 24:32], in0=st[:nr, 8:16], scalar1=1.0 / 64, scalar2=None,
                                      op0=ALU.mult), reads=[stb], writes=[stb])
        dve(lambda e: e.tensor_tensor(out=st[:nr, 32:40], in0=st[:nr, 24:32], in1=st[:nr, 24:32], op=ALU.mult),
            reads=[stb], writes=[stb])
        dve(lambda e: e.scalar_tensor_tensor(out=st[:nr, 40:48], in0=st[:nr, 16:24], scalar=1.0 / 64,
                                             in1=st[:nr, 32:40], op0=ALU.mult, op1=ALU.subtract),
            reads=[stb], writes=[stb])
        act(lambda e: e.activation(out=st[:nr, 48:56], in_=st[:nr, 40:48], func=AF.Sqrt, bias=epsT[:nr, :]),
            reads=[stb, csbb], writes=[stb])
        dve(lambda e: e.reciprocal(out=st[:nr, 56:64], in_=st[:nr, 48:56]), reads=[stb], writes=[stb])
        dve(lambda e: e.tensor_tensor(out=yb[:nr, :].rearrange("p (h d) -> p h d", h=8),
                                      in0=osb[:nr, :].rearrange("p (h d) -> p h d", h=8),
                                      in1=bc(st[:nr, 24:32].unsqueeze(2), [nr, 8, 64]), op=ALU.subtract),
            reads=[osbb, stb], writes=[ybb])
        dve(lambda e: e.tensor_tensor(out=yb[:nr, :].rearrange("p (h d) -> p h d", h=8),
                                      in0=yb[:nr, :].rearrange("p (h d) -> p h d", h=8),
                                      in1=bc(st[:nr, 56:64].unsqueeze(2), [nr, 8, 64]), op=ALU.mult),
            reads=[ybb, stb], writes=[ybb])
        if STAGE < 2.33:
            return
        pool(lambda e: e.tensor_tensor(out=yb[:nr, :], in0=yb[:nr, :], in1=gnw[:nr, :], op=ALU.mult),
             reads=[ybb, csbb], writes=[ybb])
        pool(lambda e: e.tensor_tensor(out=yb[:nr, :], in0=yb[:nr, :], in1=gnb[:nr, :], op=ALU.add),
             reads=[ybb, csbb], writes=[ybb])
        pool(lambda e: e.tensor_tensor(out=sg[o][:nr, :], in0=yb[:nr, :], in1=sg[o][:nr, :], op=ALU.mult),
             reads=[ybb, sgb[o]], writes=[sgb[o]])
        if STAGE < 2.34:
            return
        if not S_tile:
            state_update(kd[o], kdb[o], vv[o], vvb[o], nr, Sst, Sstb, Sbf, Sbfb, None, csb["cdec"])
        else:
            for b in range(4):
                dve(lambda e, b=b: e.tensor_scalar(out=km[:nr, :], in0=kd[o][:nr, :],
                                                   scalar1=csb["rowmask_s"][:nr, b:b + 1], scalar2=None,
                                                   op0=ALU.mult), reads=[kdb[o], csbb], writes=[kmb])
                state_update(km, kmb, vv[o], vvb[o], nr, Ss, Ssb, Ssbf, Ssbfb, b, csb["cdec_s"])

    def state_update(kdt, kdtb, vt, vtb, nr, S_, S_b, Sb_, Sb_b, b, cdec_):
        bd, bdb = nextbank()

        def f(e):
            last = None
            for pr in range(4):
                for w in range(2):
                    last = e.matmul(bd[:, pr * 128 + w * 64:pr * 128 + w * 64 + 64],
                                    lhsT=kdt[:nr, pr * 128:(pr + 1) * 128],
                                    rhs=vt[:nr, (2 * pr + w) * 64:(2 * pr + w + 1) * 64], start=True, stop=True)
            return last
        pe(f, reads=[kdtb, vtb], writes=[bdb])
        bdv = bd[:, :].rearrange("p (a w e) -> p a w e", a=4, w=2)
        if b is None:
            Sv = lambda lo, hi: S_[lo:hi, :, :]
            Sbv = Sb_[:, :, :]
            Sall = S_[:, :, :]
        else:
            Sv = lambda lo, hi: S_[lo:hi, b, :, :]
            Sbv = Sb_[:, b, :, :]
            Sall = S_[:, b, :, :]
        for w in range(2):
            dve(lambda e, w=w: e.tensor_tensor(out=Sv(w * 64, w * 64 + 64), in0=Sv(w * 64, w * 64 + 64),
                                               in1=bdv[w * 64:(w + 1) * 64, :, w, :], op=ALU.add),
                reads=[bdb, S_b], writes=[S_b])
        dve(lambda e: e.tensor_tensor(out=Sall, in0=Sall, in1=cdec_[:, :, :], op=ALU.mult),
            reads=[S_b, csbb], writes=[S_b])
        dve(lambda e: e.tensor_copy(out=Sbv, in_=Sall), reads=[S_b], writes=[Sb_b])

    def pv_mm(e, bank, Pget, Vget, nr, first, last_blk, heads):
        last = None
        for h in heads:
            last = e.matmul(bank[:nr, (h % 4) * 65:(h % 4) * 65 + 65], lhsT=Pget(h), rhs=Vget(h),
                            start=(first and (h % 4) == 0), stop=(last_blk and (h % 4) == 3),
                            skip_group_check=True)
        return last

    def attention_P(tl):
        nr, o = tl.nr, tl.oslot
        QTv = qk[o][:, :].rearrange("p (h t) -> p h t", h=8)
        blocks = list(range(0, min(WIN, tl.T) + 1))
        first = [True, True]
        pend = None
        for bi, j in enumerate(blocks):
            rs = ring(tl.T - j)
            par = bi % 2
            for hb in range(2):
                def f(e, hb=hb, rs=rs):
                    last = None
                    for h in range(hb * 4, hb * 4 + 4):
                        last = e.matmul(Bk[hb][:, (h % 4) * 128:(h % 4) * 128 + nr], lhsT=KT[rs][:, h, :],
                                        rhs=QTv[:, h, :nr], start=True, stop=True)
                    return last
                pe(f, reads=[KTb[rs], KTab[rs], qkb[o], qab[o]], writes=[Bkb[hb]])
            if pend is not None:
                pend()
            for hb in range(2):
                act(lambda e, hb=hb, par=par: e.activation(out=Pt[par][hb][:, :], in_=Bk[hb][:, :], func=AF.Exp),
                    reads=[Bkb[hb]], writes=[Ptb[par][hb]])
                dve(lambda e, hb=hb, par=par, j=j: e.tensor_tensor(
                    out=Pt[par][hb][:, :].rearrange("p (a t) -> p a t", a=4),
                    in0=Pt[par][hb][:, :].rearrange("p (a t) -> p a t", a=4),
                    in1=bc(csb["amask"][:, j, :].unsqueeze(1), [128, 4, 128]), op=ALU.mult),
                    reads=[Ptb[par][hb], csbb], writes=[Ptb[par][hb]])
            lastb = bi == len(blocks) - 1

            def mk(par=par, rs=rs, lastb=lastb):
                def go():
                    for hb in range(2):
                        fst = first[hb]
                        first[hb] = False
                        pe(lambda e, hb=hb, fst=fst: pv_mm(e, Bk[2 + hb],
                                                           lambda h: Pt[par][hb][:, (h % 4) * 128:(h % 4) * 128 + nr],
                                                           lambda h: VR[rs][:, h, :], nr, fst, lastb,
                                                           range(hb * 4, hb * 4 + 4)),
                           reads=[Ptb[par][hb], VRb[rs]], writes=[Bkb[2 + hb]])
                return go
            pend = mk()
            yield
        pend()
        attn_finish(nr, 2)
        mix_transpose(tl)
        yield

    def attn_finish(nr, b0):
        for hb in range(2):
            av_ = Bk[b0 + hb][:nr, 0:260].rearrange("p (a e) -> p a e", a=4)
            dve(lambda e, av_=av_, hb=hb: e.tensor_scalar(out=st[:nr, hb * 4:hb * 4 + 4].unsqueeze(2),
                                                          in0=av_[:, :, 64:65], scalar1=1e-30, scalar2=None,
                                                          op0=ALU.max), reads=[Bkb[b0 + hb]], writes=[stb])
            dve(lambda e, hb=hb: e.reciprocal(out=st[:nr, hb * 4:hb * 4 + 4], in_=st[:nr, hb * 4:hb * 4 + 4]),
                reads=[stb], writes=[stb])
            dve(lambda e, av_=av_, hb=hb: e.tensor_tensor(
                out=mix[:nr, 512 + hb * 256:512 + hb * 256 + 256].rearrange("p (a d) -> p a d", a=4),
                in0=av_[:, :, 0:64], in1=bc(st[:nr, hb * 4:hb * 4 + 4].unsqueeze(2), [nr, 4, 64]), op=ALU.mult),
                reads=[Bkb[b0 + hb], stb], writes=[mixb])

    def attention_S(tl):
        nr, o = tl.nr, tl.oslot
        QTv = qk[o][:, :].rearrange("p (h t) -> p h t", h=8)
        first = [True, True]

        def load_pair(b, c2, sp_, gdep=False):
            extra = gTb if gdep else []
            T.dma("sp", lambda e: e.dma_start(
                out=Kst[sp_], in_=ck_d[b, c2 * 256:(c2 + 1) * 256, :].rearrange("(a p) f -> p a f", p=128)),
                S_kst[sp_], writes=[Kstb[sp_]] + extra)
            T.dma("sp", lambda e: e.dma_start(
                out=Vst[sp_], in_=cv_d[b, c2 * 256:(c2 + 1) * 256, :].rearrange("(a p) f -> p a f", p=128)),
                S_vst[sp_], writes=[Vstb[sp_]] + extra)

        seq = [(b, ch) for b in range(4) for ch in range(16)]
        pairs = [(b, c2) for b in range(4) for c2 in range(8)]
        NB_ = len(seq)
        trs = {}

        def stage_A(i):
            b, ch = seq[i]
            sp_, blk, cp, vp, kp = (i // 2) % 2, i % 2, i % 2, i % 3, i % 2
            extra = gTb if i >= NB_ - 2 else []
            dve(lambda e: e.tensor_copy(out=Kc[cp][:, :], in_=Kst[sp_][:, blk, :]), reads=[Kstb[sp_]] + extra,
                writes=[Kcb[cp]])
            act(lambda e: e.copy(out=Vc[vp][:, :, 0:64], in_=Vst[sp_][:, blk, :].rearrange("p (h d) -> p h d", h=8)),
                reads=[Vstb[sp_]] + extra, writes=[Vcb[vp]])
            if blk == 1 and i // 2 + 2 < len(pairs):
                pb, pc2 = pairs[i // 2 + 2]
                load_pair(pb, pc2, sp_)
            tr, trb = nexttr()

            def f(e):
                last = None
                for h in range(8):
                    last = e.transpose(out=tr[0:64, h * 128:(h + 1) * 128], in_=Kc[cp][:, h * 64:(h + 1) * 64],
                                       identity=csb["ident_bf"][:, :])
                return last
            pe(f, reads=[Kcb[cp], csbb], writes=[trb])
            dve(lambda e: e.tensor_copy(out=KTs[kp][0:64, :, :], in_=tr[0:64, :].rearrange("p (h t) -> p h t", h=8)),
                reads=[trb], writes=[KTsb[kp]])
            dve(lambda e: e.tensor_scalar(out=KTs[kp][64:68, :, :],
                                          in0=bc(csb["kA_base"][64:68, :].unsqueeze(1), [4, 8, 128]),
                                          scalar1=csb["ecb"][64:68, ch:ch + 1], scalar2=None, op0=ALU.add),
                reads=[csbb], writes=[KTsab[kp]])

        def stage_B(i):
            b, ch = seq[i]
            kp, pp, q0 = i % 2, i % 3, b * 10 + 2
            if ch == 0 and i > 0:
                pass
            for hb in range(2):
                def f2(e, hb=hb):
                    last = None
                    for h in range(hb * 4, hb * 4 + 4):
                        last = e.matmul(Bk[hb][:, (h % 4) * 128:(h % 4) * 128 + 8], lhsT=KTs[kp][:, h, :],
                                        rhs=QTv[:, h, q0:q0 + 8], start=True, stop=True)
                    return last
                pe(f2, reads=[KTsb[kp], KTsab[kp], qkb[o], qab[o]], writes=[Bkb[hb]])
            if ch < 3:
                pool(lambda e: e.memset(Pf[pp][:], 0.0), writes=[Pfb[pp]])
            for hb in range(2):
                act(lambda e, hb=hb: e.activation(
                    out=Pf[pp][:, hb * 4:hb * 4 + 4, q0:q0 + 8],
                    in_=Bk[hb][:, :].rearrange("p (a t) -> p a t", a=4)[:, :, 0:8], func=AF.Exp),
                    reads=[Bkb[hb]], writes=[Pfb[pp]])
            dve(lambda e: e.tensor_tensor(
                out=Pf[pp][:, :, q0:q0 + 8], in0=Pf[pp][:, :, q0:q0 + 8],
                in1=bc(csb["amask_s"][:, ch, :].unsqueeze(1), [128, 8, 8]), op=ALU.mult),
                reads=[Pfb[pp], csbb], writes=[Pfb[pp]])

        def stage_C(i):
            pp, vp = i % 3, i % 3
            for hb in range(2):
                fst = first[hb]
                first[hb] = False
                pe(lambda e, hb=hb, fst=fst: pv_mm(
                    e, Bk[4 + hb], lambda h: Pf[pp][:, h, :NS], lambda h: Vc[vp][:, h, :], NS, fst, False,
                    range(hb * 4, hb * 4 + 4)),
                    reads=[Pfb[pp], Vcb[vp]], writes=[Bkb[4 + hb]])

        load_pair(pairs[0][0], pairs[0][1], 0, gdep=True)
        load_pair(pairs[1][0], pairs[1][1], 1)
        for step in range(NB_ + 2):
            if step < NB_:
                stage_A(step)
            if 0 <= step - 1 < NB_:
                stage_B(step - 1)
            if 0 <= step - 2 < NB_:
                stage_C(step - 2)
            yield
        for hb in range(2):
            def f3(e, hb=hb):
                last = None
                for h in range(hb * 4, hb * 4 + 4):
                    last = e.matmul(Bk[hb][:NS, (h % 4) * 128:(h % 4) * 128 + NS], lhsT=KTn[:, h, :NS],
                                    rhs=QTv[:, h, :NS], start=True, stop=True)
                return last
            pe(f3, reads=[KTnb, qkb[o], qab[o]], writes=[Bkb[hb]])
        for hb in range(2):
            act(lambda e, hb=hb: e.activation(
                out=Pf[0][:NS, hb * 4:hb * 4 + 4, :NS],
                in_=Bk[hb][:NS, :].rearrange("p (a t) -> p a t", a=4)[:, :, :NS], func=AF.Exp),
                reads=[Bkb[hb]], writes=[Pfb[0]])
        dve(lambda e: e.tensor_tensor(
            out=Pf[0][:NS, :, :NS], in0=Pf[0][:NS, :, :NS],
            in1=bc(csb["mask_n"][:NS, :NS].unsqueeze(1), [NS, 8, NS]), op=ALU.mult),
            reads=[Pfb[0], csbb], writes=[Pfb[0]])
        for hb in range(2):
            fst = first[hb]
            first[hb] = False
            pe(lambda e, hb=hb, fst=fst: pv_mm(e, Bk[4 + hb], lambda h: Pf[0][:NS, h, :NS],
                                               lambda h: Vn[:NS, h, :], NS, fst, True, range(hb * 4, hb * 4 + 4)),
               reads=[Pfb[0], Vnb], writes=[Bkb[4 + hb]])
        yield "done"

    def mix_transpose(tl):
        nr, o = tl.nr, tl.oslot
        tr, trb = nexttr()

        def f(e):
            last = None
            for k in range(8):
                src = sg[o][:nr, k * 128:(k + 1) * 128] if k < 4 else mix[:nr, k * 128:(k + 1) * 128]
                last = e.transpose(out=tr[:, k * 128:k * 128 + nr], in_=src, identity=csb["ident_bf"][:nr, :nr])
            return last
        pe(f, reads=[mixb, sgb[o], csbb], writes=[trb])
        act(lambda e: e.copy(out=hT[o][:, :, :nr], in_=tr[:].rearrange("p (k t) -> p k t", k=8)[:, :, :nr]),
            reads=[trb], writes=[hTb[o]])

    tile_ctr = [0]

    def new_tile(kind, Tt):
        tl = Tile()
        tl.kind = kind
        tl.T = Tt
        tl.nr = NS if kind == "S" else 128
        i = tile_ctr[0]
        tile_ctr[0] += 1
        tl.xslot = i % NXS
        tl.oslot = i % NOS
        tl.post = []
        tl.hbuf = hT[tl.oslot]
        tl.hbufb = hTb[tl.oslot]
        return tl

    all_groups = []
    for gi in range(n_pre_groups):
        tl_list = [new_tile("P", gi * G + g) for g in range(G)]
        if (n_pre_groups - gi) % 2 == 1:
            for g_, tl_ in enumerate(tl_list):
                tl_.hbuf = h2T[:, :, g_ * 128:(g_ + 1) * 128]
                tl_.hbufb = h2Tb[g_]
        all_groups.append(("pre", tl_list))
    main_tiles = [("P", NT_PRE + m) for m in range(NT_MAIN)] + [("S", None)]
    for gi in range(n_main_groups):
        all_groups.append(("main", [new_tile(k_, t_) for (k_, t_) in main_tiles[gi * G:(gi + 1) * G]]))

    def issue_loads(grp):
        for tl in grp[1]:
            load_x(tl)

    prep_group(all_groups[0])
    late_setup()

    def do_group(gidx, gkind, tiles):
        for g_, tl_ in enumerate(tiles):
            tl_.g = g_
        if STAGE < 1 or (STAGE < 2 and gkind == "main"):
            return
        nxt = all_groups[gidx + 1] if gidx + 1 < len(all_groups) else None
        if gkind == "pre":
            slabs = (1, 2, 5, 6)
        else:
            slabs = tuple(range(7))
        prev = []
        for slab_i, c in enumerate(slabs):
            wv_, wb_ = take_slab("in", c)
            cur = []
            for tl in tiles:
                bk, bkb = nextbank()
                proj_mm(tl, wv_, wb_, bk, bkb, tl.hbuf, tl.hbufb)
                cur.append((tl, bk, bkb))
            prefetch_slab()
            if gkind == "pre" and nxt is not None and slab_i < len(nxt[1]):
                prep_tile(nxt[1][slab_i])
            for tl in tiles:
                run_post(tl)
            if gkind == "main" and c == 4:
                pass
            for (tl, bk, bkb) in cur:
                evac_slab(tl, c, bk, bkb, full=(gkind == "main"))
            if gkind == "main" and c == 3:
                for tl in tiles:
                    run_post(tl)
                if STAGE < 2.2:
                    return
                for tl in tiles:
                    if tl.kind == "S" and STAGE < 2.5:
                        continue
                    retention(tl)
                if STAGE < 2.7:
                    return
        for tl in tiles:
            run_post(tl)
        if gkind == "pre":
            for tl in tiles:
                state_update(kd[tl.oslot], kdb[tl.oslot], vv[tl.oslot], vvb[tl.oslot], 128, Sst, Sstb, Sbf, Sbfb,
                             None, csb["cdec"])
            return
        if STAGE < 3:
            return
        for tl in tiles:
            load_x(tl)
        mg_ = gidx - n_pre_groups
        bs_ = [mg_] if mg_ < 4 else []
        if mg_ == n_main_groups - 1:
            bs_ += [b_ for b_ in range(4) if b_ >= n_main_groups]
        for b_ in bs_:
            T.dma("pool", lambda e, b_=b_: [e.dma_start(out=wks_d[b_, 0:2040, :], in_=ck_d[b_, 8:2048, :]),
                                          e.dma_start(out=wvs_d[b_, 0:2040, :], in_=cv_d[b_, 8:2048, :])],
                  S_out, n=2, is_output=True)
        s_tl = [tl for tl in tiles if tl.kind == "S"]
        gens_P = [attention_P(tl) for tl in tiles if tl.kind == "P"]
        gen_S = attention_S(s_tl[0]) if s_tl else None
        s_done = gen_S is None
        for gp in gens_P:
            for _ in gp:
                for _k in range(2):
                    if not s_done:
                        if next(gen_S, "done") == "done":
                            s_done = True
        while not s_done:
            if next(gen_S, "done") == "done":
                s_done = True
        if s_tl:
            attn_finish(NS, 4)
            mix_transpose(s_tl[0])
        if STAGE < 4:
            return
        for cbk in range(2):
            wv_, wb_ = take_slab("out", cbk)
            for tl in tiles:
                bk, bkb = nextbank()
                proj_mm(tl, wv_, wb_, bk, bkb, hT[tl.oslot], hTb[tl.oslot])
                xs = tl.xslot
                dve(lambda e, tl=tl, bk=bk, cbk=cbk, xs=xs: e.tensor_tensor(
                    out=xs_[xs][:tl.nr, cbk * 512:(cbk + 1) * 512], in0=bk[:tl.nr, :],
                    in1=xs_[xs][:tl.nr, cbk * 512:(cbk + 1) * 512], op=ALU.add),
                    reads=[bkb, xb[xs]], writes=[xb[xs]])
            prefetch_slab()
        col = 0
        for g, tl in enumerate(tiles):
            tl.col = col
            rmsnorm_T(tl, xs_[tl.xslot], [xb[tl.xslot]], nw2,
                      lambda nr, c0=col: h2T[:, :, c0:c0 + nr], h2Tb[g])
            col += tl.nr
        N = col
        has_S = tiles[-1].kind == "S"
        lastP = [tl for tl in tiles if tl.kind == "P"][-1]
        lp_end = lastP.col + 128
        if STAGE < 5:
            return
        def up_part1(c, wU, wUb, kc):
            p2 = c % 2

            def fu(e, off, bank):
                last = None
                for k in range(8):
                    last = e.matmul(bank[:, :N], lhsT=wU[:, off + k, kc * 128:(kc + 1) * 128], rhs=h2T[:, k, :N],
                                    start=(k == 0), stop=(k == 7))
                return last
            pe(lambda e: fu(e, 0, Bk[p2]), reads=[wUb] + h2Tb, writes=[Bkb[p2]])
            pe(lambda e: fu(e, 8, Bk[2 + p2]), reads=[wUb] + h2Tb, writes=[Bkb[2 + p2]])
            ub_ = uab[p2]
            act(lambda e: e.copy(out=ub_[:, 2:2 + N], in_=Bk[p2][:, :N]), reads=[Bkb[p2]], writes=[uabb[p2]])
            pool(lambda e: e.tensor_copy(out=ub_[:, 0:2], in_=carry[:, c, :]), reads=[carryb], writes=[uabb[p2]])
            if has_S:
                so = tiles[-1].col
                pool(lambda e: e.tensor_copy(
                    out=ub_[:, 2 + so:2 + so + NS].rearrange("p (b t) -> p b t", b=4)[:, :, 0:2],
                    in_=sconvT[:, :, c].rearrange("p (b j) -> p b j", b=4)),
                    reads=[sconvTb, uabb[p2]], writes=[uabb[p2]])
            act(lambda e: e.activation(out=cvt[p2][:, :N], in_=ub_[:, 0:N], func=AF.Identity,
                                       scale=convfm[:, c, 0:1], bias=convfm[:, c, 3:4]),
                reads=[uabb[p2], csbb], writes=[cvtb[p2]])
            for jj in (1, 2):
                dve(lambda e, jj=jj: e.scalar_tensor_tensor(
                    out=cvt[p2][:, :N], in0=ub_[:, jj:jj + N], scalar=convfm[:, c, jj:jj + 1],
                    in1=cvt[p2][:, :N], op0=ALU.mult, op1=ALU.add), reads=[uabb[p2], csbb, cvtb[p2]],
                    writes=[cvtb[p2]])

        def up_part2(c):
            p2 = c % 2
            ub_ = uab[p2]
            act(lambda e: e.activation(out=cvt[p2][:, :N], in_=cvt[p2][:, :N], func=AF.Silu),
                reads=[cvtb[p2]], writes=[cvtb[p2]])
            dve(lambda e: e.tensor_tensor(out=gT[:, c, :N], in0=Bk[2 + p2][:, :N], in1=cvt[p2][:, :N], op=ALU.mult),
                reads=[Bkb[2 + p2], cvtb[p2]], writes=[gTb[c]])
            pool(lambda e: e.tensor_copy(out=carry[:, c, :], in_=ub_[:, lp_end:lp_end + 2]),
                 reads=[uabb[p2]], writes=[carryb])
            if lastP.T == NT - 1:
                pool(lambda e: e.tensor_copy(out=convp_st[:, :, c], in_=ub_[:, lp_end:lp_end + 2]),
                     reads=[uabb[p2]], writes=[convp_stb])
            if has_S:
                so = tiles[-1].col
                pool(lambda e: e.tensor_copy(
                    out=convs_st[:, :, c].rearrange("p (b j) -> p b j", b=4),
                    in_=ub_[:, 2 + so:2 + so + NS].rearrange("p (b t) -> p b t", b=4)[:, :, 8:10]),
                    reads=[uabb[p2]], writes=[convs_stb])

        for u_ in range(11):
            if nxt is not None:
                k_, ph_ = divmod(u_, 3)
                if ph_ == 0 and 1 <= k_ <= len(nxt[1]):
                    prep_norm(nxt[1][k_ - 1], "b")
                if k_ < len(nxt[1]):
                    if ph_ == 0:
                        load_x_stage(nxt[1][k_])
                    elif ph_ == 1:
                        prep_norm(nxt[1][k_], "a")
            wU, wUb = take_slab("up", u_)
            for kc in range(2):
                c = u_ * 2 + kc
                up_part1(c, wU, wUb, kc)
                if c >= 1:
                    up_part2(c - 1)
            prefetch_slab()
        up_part2(NCH - 1)
        if STAGE < 6:
            return
        for s_ in range(6):
            wD, wDb = take_slab("dn", s_)
            nchunks = wD.shape[1]
            for g, tl in enumerate(tiles):
                for cbk in range(2):
                    bi = g * 2 + cbk

                    def fd(e, tl=tl, cbk=cbk, bi=bi, wD=wD, nchunks=nchunks, s_=s_):
                        last = None
                        for kc in range(nchunks):
                            c = s_ * 4 + kc
                            last = e.matmul(Bk[bi][:tl.nr, :], lhsT=gT[:, c, tl.col:tl.col + tl.nr],
                                            rhs=wD[:, kc, cbk * 512:(cbk + 1) * 512], start=(c == 0),
                                            stop=(c == NCH - 1))
                        return last
                    pe(fd, reads=[wDb] + gTb[s_ * 4:s_ * 4 + nchunks], writes=[Bkb[bi]])
            prefetch_slab()
        for g, tl in enumerate(tiles):
            xs, nr = tl.xslot, tl.nr
            for cbk in range(2):
                bi = g * 2 + cbk
                dve(lambda e, xs=xs, nr=nr, cbk=cbk, bi=bi: e.tensor_tensor(
                    out=xs_[xs][:nr, cbk * 512:(cbk + 1) * 512], in0=Bk[bi][:nr, :],
                    in1=xs_[xs][:nr, cbk * 512:(cbk + 1) * 512], op=ALU.add),
                    reads=[Bkb[bi], xb[xs]], writes=[xb[xs]])
            act(lambda e, xs=xs, nr=nr: e.activation(out=xn[:nr, :], in_=xs_[xs][:nr, :], func=AF.Square,
                                                     accum_out=st[:nr, 0:1]), reads=[xb[xs]], writes=[xnb, stb])
            act(lambda e, nr=nr: e.activation(out=st[:nr, 1:2], in_=st[:nr, 0:1], func=AF.Sqrt, scale=1.0 / D,
                                              bias=epsT[:nr, :]), reads=[stb, csbb], writes=[stb])
            dve(lambda e, nr=nr: e.reciprocal(out=st[:nr, 2:3], in_=st[:nr, 1:2]), reads=[stb], writes=[stb])
            dve(lambda e, xs=xs, nr=nr: e.scalar_tensor_tensor(out=xs_[xs][:nr, :], in0=xs_[xs][:nr, :],
                                                               scalar=st[:nr, 2:3], in1=nwf[:nr, :], op0=ALU.mult,
                                                               op1=ALU.mult), reads=[xb[xs], stb, csbb],
                writes=[xb[xs]])
            if tl.kind == "S":
                T.dma("pool", lambda e, xs=xs: e.dma_start(out=ys_d[:, :], in_=xs_[xs][:NS, :]), S_y[xs],
                      reads=[xb[xs]], is_output=True)
            else:
                m = tl.T - NT_PRE
                T.dma("pool", lambda e, xs=xs, m=m: e.dma_start(out=y_d[m * 128:(m + 1) * 128, :], in_=xs_[xs][:, :]),
                      S_y[xs], reads=[xb[xs]], is_output=True)
    for gidx_, (gkind_, tiles_) in enumerate(all_groups):
        do_group(gidx_, gkind_, tiles_)

    if STAGE < 7:
        T.final_waits("pool")
        return _materialize(nc, T, es)
    T.dma("pool", lambda e: [e.dma_start(out=ret_d, in_=Sst[:, :, :]), e.dma_start(out=rets_d, in_=Ss[:, :, :, :])],
          S_fin, reads=[Sstb, Ssb], n=2, is_output=True)
    for (src, srcb, nj, dst) in ((convp_st, convp_stb, 2, convp_d), (convs_st, convs_stb, 8, convs_d)):
        nfl = nj * NCH
        srcf = src[:, :, :].rearrange("p j c -> p (j c)")
        dflat = dst.rearrange("j (c f) -> (j c) f", f=128)
        bk, bkb = nextbank()

        def f(e, srcf=srcf, nfl=nfl, bk=bk):
            n0 = min(128, nfl)
            last = e.transpose(out=bk[0:n0, 0:128], in_=srcf[:, 0:n0], identity=csb["ident_f"][:, :])
            if nfl > 128:
                last = e.transpose(out=bk[0:nfl - 128, 128:256], in_=srcf[:, 128:nfl], identity=csb["ident_f"][:, :])
            return last
        pe(f, reads=[srcb, csbb], writes=[bkb])
        n0 = min(128, nfl)
        act(lambda e, bk=bk, n0=n0: e.copy(out=so_tok[0:n0, 0, :], in_=bk[0:n0, 0:128]), reads=[bkb],
            writes=[so_tokb])
        if nfl > 128:
            act(lambda e, bk=bk, nfl=nfl: e.copy(out=so_tok[0:nfl - 128, 1, :], in_=bk[0:nfl - 128, 128:256]),
                reads=[bkb], writes=[so_tokb])

        def fo(e, dflat=dflat, nfl=nfl, n0=n0):
            r = [e.dma_start(out=dflat[0:n0, :], in_=so_tok[0:n0, 0, :])]
            if nfl > 128:
                r.append(e.dma_start(out=dflat[128:nfl, :], in_=so_tok[0:nfl - 128, 1, :]))
            return r
        T.dma("pool", fo, S_sotok, reads=[so_tokb], n=(2 if nfl > 128 else 1), is_output=True)
    T.final_waits("pool")

    return _materialize(nc, T, es)


def _materialize(nc, T, es):
    esem = {e_: es.enter_context(nc.semaphore("sem_" + e_)) for e_ in Tracker.ENG}
    dsem = [es.enter_context(nc.semaphore(f"dsem{i}")) for i in range(len(T.dma_count))]

    def semof(sk):
        return esem[sk[1]] if sk[0] == "e" else dsem[sk[1]]

    def run_stream(name, eng):
        for ent in T.streams[name]:
            if ent[0] == "wait":
                eng.wait_ge(semof(ent[1]), ent[2])
            else:
                _, emit, sk, inc = ent
                r = emit(eng)
                if isinstance(r, (list, tuple)):
                    for ins in r:
                        ins.then_inc(semof(sk), inc)
                else:
                    r.then_inc(semof(sk), inc)

    with nc.Block() as block:
        @block.tensor
        def _(e):
            run_stream("pe", e)

        @block.scalar
        def _(e):
            run_stream("act", e)

        @block.vector
        def _(e):
            run_stream("dve", e)

        @block.gpsimd
        def _(e):
            run_stream("pool", e)

        @block.sync
        def _(e):
            run_stream("sp", e)
    es.close()
    return nc


def _common_inputs(norm1_w, w_in, ret_gn_w, ret_gn_b, w_out, norm2_w, w_up, conv_w, conv_b, w_down, normf_w, NT):
    f = np.float32
    m = {
        "w_in": np.ascontiguousarray(w_in, f), "w_out": np.ascontiguousarray(w_out, f),
        "w_up": np.ascontiguousarray(w_up, f), "w_down": np.ascontiguousarray(w_down, f),
        "gnw_t": np.ascontiguousarray(np.broadcast_to(np.asarray(ret_gn_w, f)[None, :], (128, 512))),
        "gnb_t": np.ascontiguousarray(np.broadcast_to(np.asarray(ret_gn_b, f)[None, :], (128, 512))),
        "nwf_t": np.ascontiguousarray(np.broadcast_to(np.asarray(normf_w, f)[None, :], (128, D))),
        "nw1fm": np.ascontiguousarray(np.asarray(norm1_w, f).reshape(8, 128).T),
        "nw2fm": np.ascontiguousarray(np.asarray(norm2_w, f).reshape(8, 128).T),
    }
    cf = np.concatenate([np.asarray(conv_w, f), np.asarray(conv_b, f)[None, :]], axis=0)
    m["convfm"] = np.ascontiguousarray(cf.reshape(4, NCH, 128).transpose(2, 1, 0))
    for k, v in _consts(NT).items():
        m["c_" + k] = np.ascontiguousarray(v)
    return m


def _sample_inputs(x_sample, state_ret, cache_win_k, cache_win_v, state_conv, c):
    f = np.float32
    xs = np.zeros((NS, D), f)
    for b in range(4):
        xs[b * 10 + 2:b * 10 + 10] = x_sample[4 * c + b]
    return {
        "xs": xs,
        "sret": np.ascontiguousarray(state_ret[4 * c:4 * c + 4], f),
        "ck": np.ascontiguousarray(np.asarray(cache_win_k[4 * c:4 * c + 4], f).reshape(4, 2048, 512)),
        "cv": np.ascontiguousarray(np.asarray(cache_win_v[4 * c:4 * c + 4], f).reshape(4, 2048, 512)),
        "sconv": np.ascontiguousarray(np.asarray(state_conv[4 * c:4 * c + 4], f).reshape(8, DFF)),
    }


def _unpack_state(a):
    return np.ascontiguousarray(a.reshape(2, 64, 4, 64).transpose(2, 0, 1, 3).reshape(8, 64, 64))


_PROG_CACHE = {}


def kernel(x_prompt, x_sample, state_ret, cache_win_k, cache_win_v, state_conv, norm1_w, w_in, ret_gn_w,
           ret_gn_b, w_out, norm2_w, w_up, conv_w, conv_b, w_down, normf_w):
    NT_PRE, NT_MAIN = 15, 17
    NT = NT_PRE + NT_MAIN
    x_prompt = np.asarray(x_prompt, np.float32)
    x_sample = np.asarray(x_sample, np.float32)
    key = (NT_PRE, NT_MAIN)
    if key not in _PROG_CACHE:
        _PROG_CACHE[key] = build_program(NT_PRE, NT_MAIN, out_first_kv_tile=16)
    nc = _PROG_CACHE[key]
    common = _common_inputs(norm1_w, w_in, ret_gn_w, ret_gn_b, w_out, norm2_w, w_up, conv_w, conv_b, w_down,
                            normf_w, NT)
    in_maps = []
    for c in range(8):
        b, role = c // 2, c % 2
        x = np.zeros((NT * 128, D), np.float32)
        vf = np.zeros((128, NT), np.float32)
        if role == 0:
            x[16 * 128:] = x_prompt[b, 0:2048]
            vf[:, 16:] = 1.0
        else:
            x[:] = x_prompt[b]
            vf[:] = 1.0
        m = dict(common)
        m["x"] = x
        m["vflag"] = vf.astype(NPBF)
        m.update(_sample_inputs(x_sample, state_ret, cache_win_k, cache_win_v, state_conv, c))
        in_maps.append(m)
    res = run_bass_kernel_spmd(nc, in_maps, core_ids=list(range(8)))
    R = res.results
    y_prompt = np.zeros((4, 4096, D), np.float32)
    y_sample = np.zeros((32, 8, D), np.float32)
    ret_p = np.zeros((4, 8, 64, 64), np.float32)
    ret_s = np.zeros((32, 8, 64, 64), np.float32)
    wk_p = np.zeros((4, 2048, 8, 64), np.float32)
    wv_p = np.zeros((4, 2048, 8, 64), np.float32)
    wk_s = np.zeros((32, 2048, 8, 64), np.float32)
    wv_s = np.zeros((32, 2048, 8, 64), np.float32)
    conv_p = np.zeros((4, 2, DFF), np.float32)
    conv_s = np.zeros((32, 2, DFF), np.float32)
    for c in range(8):
        b, role = c // 2, c % 2
        r = R[c]
        y_prompt[b, role * 2048:(role + 1) * 2048] = r["y"][128:]
        for bb in range(4):
            y_sample[4 * c + bb] = r["ys"][bb * 10 + 2:bb * 10 + 10]
            ret_s[4 * c + bb] = _unpack_state(r["rets"][:, bb])
        wk_s[4 * c:4 * c + 4] = r["wks"].reshape(4, 2048, 8, 64)
        wv_s[4 * c:4 * c + 4] = r["wvs"].reshape(4, 2048, 8, 64)
        conv_s[4 * c:4 * c + 4] = r["convs"].reshape(4, 2, DFF)
        if role == 1:
            ret_p[b] = _unpack_state(r["ret"])
            wk_p[b] = r["wk"].reshape(2048, 8, 64)
            wv_p[b] = r["wv"].reshape(2048, 8, 64)
            conv_p[b] = r["convp"]
    return (y_prompt, y_sample, ret_p, ret_s, wk_p, wv_p, wk_s, wv_s, conv_p, conv_s)
```

```python
import math
from contextlib import ExitStack

import numpy as np
import ml_dtypes

import concourse.bass as bass
import concourse.mybir as mybir
from concourse.bass_utils import run_bass_kernel_spmd

F32 = mybir.dt.float32
BF16 = mybir.dt.bfloat16
AF = mybir.ActivationFunctionType
ALU = mybir.AluOpType
AX = mybir.AxisListType
NPBF = ml_dtypes.bfloat16

D = 1024
H = 8
DFF = 2816
NCH = DFF // 128
EPS = 1e-6
NS = 40
G = 3
WIN = 16
NRING = WIN + G
NWS = 3
NXS = 4
NOS = 3


class Buf:
    __slots__ = ("name", "w", "r")

    def __init__(self, name):
        self.name = name
        self.w = None
        self.r = []


class Tracker:
    ENG = ("pe", "act", "dve", "pool", "sp")

    def __init__(self):
        self.streams = {e: [] for e in self.ENG}
        self.count = {e: 0 for e in self.ENG}
        self.waited = {e: {} for e in self.ENG}
        self.dma_count = []
        self.out_events = []

    def new_dma_sem(self):
        self.dma_count.append(0)
        return ("d", len(self.dma_count) - 1)

    def _deps(self, reads, writes):
        deps = []
        for b in reads:
            if b.w is not None:
                deps.append(b.w)
        for b in writes:
            if b.w is not None:
                deps.append(b.w)
            deps.extend(b.r)
        return deps

    def _emit_waits(self, eng, deps, is_dma):
        st = self.streams[eng]
        wd = self.waited[eng]
        best = {}
        for sk, val in deps:
            if sk == ("e", eng) and not is_dma:
                if eng == "pe":
                    continue
                if val <= self.count[eng] - 2:
                    continue
            if wd.get(sk, 0) >= val:
                continue
            if best.get(sk, 0) < val:
                best[sk] = val
        for sk, val in best.items():
            st.append(("wait", sk, val))
            wd[sk] = val

    def op(self, eng, emit, reads=(), writes=()):
        deps = self._deps(reads, writes)
        self._emit_waits(eng, deps, False)
        self.count[eng] += 1
        ev = (("e", eng), self.count[eng])
        self.streams[eng].append(("inst", emit, ev[0], 1))
        for b in reads:
            b.r.append(ev)
        for b in writes:
            b.w = ev
            b.r = []
        return ev

    def dma(self, eng, emit, sem, reads=(), writes=(), n=1, is_output=False):
        deps = self._deps(reads, writes)
        self._emit_waits(eng, deps, True)
        self.dma_count[sem[1]] += 16 * n
        ev = (sem, self.dma_count[sem[1]])
        self.streams[eng].append(("inst", emit, sem, 16))
        for b in reads:
            b.r.append(ev)
        for b in writes:
            b.w = ev
            b.r = []
        if is_output:
            self.out_events.append(ev)
        return ev

    def final_waits(self, eng):
        best = {}
        for sk, val in self.out_events:
            if best.get(sk, 0) < val:
                best[sk] = val
        for sk, val in best.items():
            self.streams[eng].append(("wait", sk, val))


def _mult(d):
    d = np.asarray(d)
    c = ((d >= 0) & (d <= 128)).astype(np.float32)
    c += ((d >= 0) & (d <= 512) & (d % 4 == 0)).astype(np.float32)
    c += ((d >= 0) & (d <= 2048) & (d % 16 == 0)).astype(np.float32)
    return c


def _consts(ntiles_total):
    lg = np.log1p(-np.exp2(-5.0 - np.arange(H, dtype=np.float64)))
    slopes = np.exp2(-8.0 * np.arange(1, H + 1, dtype=np.float64) / H)
    c = {}
    c["ident_bf"] = np.eye(128, dtype=np.float32).astype(NPBF)
    c["ident_f"] = np.eye(128, dtype=np.float32)
    i = np.arange(128, dtype=np.float64)
    c["qdec"] = np.exp(lg[None, :] * (i[:, None] + 1.0)).astype(np.float32)
    c["kdec"] = (np.exp(-lg[None, :] * (i[:, None] + 1.0)) * 0.125).astype(np.float32)
    c["maskT"] = (i[None, :] >= i[:, None]).astype(np.float32)
    cd = np.zeros((128, 4, 64), np.float32)
    cds = np.zeros((128, 4, 64), np.float32)
    for p in range(128):
        for pr in range(4):
            h = 2 * pr + (p // 64)
            cd[p, pr, :] = math.exp(lg[h] * 128.0)
            cds[p, pr, :] = math.exp(lg[h] * 8.0)
    c["cdec"] = cd
    c["cdec_s"] = cds
    qs = np.zeros((128, 8), np.float32)
    ks = np.zeros((128, 8), np.float32)
    valid = np.zeros(NS, bool)
    bat = np.zeros(NS, int)
    tt = np.zeros(NS, int)
    for b in range(4):
        for t in range(8):
            r = b * 10 + 2 + t
            valid[r] = True
            tt[r] = t
            qs[r] = np.exp(lg * (t + 1.0))
            ks[r] = np.exp(-lg * (t + 1.0)) * 0.125
        bat[b * 10:(b + 1) * 10] = b
    c["qdec_s"] = qs
    c["kdec_s"] = ks
    ms = np.zeros((128, 128), np.float32)
    mn = np.zeros((128, 128), np.float32)
    for j in range(NS):
        for q in range(NS):
            if valid[j] and valid[q] and bat[j] == bat[q] and tt[q] >= tt[j]:
                ms[j, q] = 1.0
                mn[j, q] = _mult(tt[q] - tt[j])
    c["maskT_s"] = ms
    c["mask_n"] = mn.astype(NPBF)
    colm = np.zeros((128, 4, NS), np.float32)
    rowm = np.zeros((128, 4), np.float32)
    for b in range(4):
        colm[:, b, b * 10 + 2:b * 10 + 10] = 1.0
        rowm[b * 10 + 2:b * 10 + 10, b] = 1.0
    c["colmask_s"] = colm.astype(NPBF)
    c["rowmask_s"] = rowm
    vfs = np.zeros((128, 1), np.float32)
    vfs[:NS, 0] = valid
    c["vflag_s"] = vfs.astype(NPBF)
    k = np.arange(128)[:, None, None]
    jj = np.arange(WIN + 1)[None, :, None]
    q = np.arange(128)[None, None, :]
    c["amask"] = _mult(q - k + 128 * jj).astype(NPBF)
    cb = np.arange(16)[None, :, None]
    t8 = np.arange(8)[None, None, :]
    c["amask_s"] = _mult(2048 + t8 - (128 * cb + k)).astype(NPBF)
    qa = np.zeros((ntiles_total, 4, H, 128), np.float32)
    ka = np.zeros((ntiles_total, 4, H, 128), np.float32)
    loc = np.arange(128, dtype=np.float64)
    for T in range(ntiles_total):
        for h in range(H):
            qa[T, 0, h] = -slopes[h] * loc
            qa[T, 1, h] = slopes[h]
            qa[T, 2, h] = slopes[h] * 128.0
            qa[T, 3, h] = -slopes[h] * 128.0 * T
            ka[T, 0, h] = 1.0
            ka[T, 1, h] = loc
            ka[T, 2, h] = T
            ka[T, 3, h] = 1.0
    c["qaug"] = qa.astype(NPBF)
    c["kaug"] = ka.astype(NPBF)
    qas = np.zeros((4, H, NS), np.float32)
    kan = np.zeros((4, H, NS), np.float32)
    for h in range(H):
        qas[0, h] = -slopes[h] * tt
        qas[1, h] = slopes[h]
        qas[2, h] = slopes[h] * 128.0
        qas[3, h] = -slopes[h] * 128.0 * 16
        kan[0, h] = 1.0
        kan[1, h] = tt
        kan[2, h] = 16
        kan[3, h] = 1.0
    c["qaug_s"] = qas.astype(NPBF)
    c["kaug_n"] = kan.astype(NPBF)
    kas = np.zeros((16, 4, H, 128), np.float32)
    for cbi in range(16):
        kas[cbi, 0] = 1.0
        kas[cbi, 1] = loc[None, :]
        kas[cbi, 2] = cbi
        kas[cbi, 3] = 1.0
    c["kaug_s"] = kas.astype(NPBF)
    kab = np.zeros((128, 128), np.float32)
    kab[64] = 1.0
    kab[65] = loc
    kab[67] = 1.0
    c["kA_base"] = kab.astype(NPBF)
    ecb = np.zeros((128, 16), np.float32)
    ecb[66] = np.arange(16)
    c["ecb"] = ecb
    return c


CONST_SHAPES = None


def build_program(NT_PRE, NT_MAIN, out_first_kv_tile, STAGE=99):
    assert NT_PRE % G == 0 and (NT_MAIN + 1) % G == 0
    NT = NT_PRE + NT_MAIN
    nc = bass.Bass("TRN2", target_bir_lowering=False)
    T = Tracker()
    es = ExitStack()

    def din(name, shape, dt=F32):
        return nc.dram_tensor(name, list(shape), dt, kind="ExternalInput").ap()

    def dout(name, shape, dt=F32):
        return nc.dram_tensor(name, list(shape), dt, kind="ExternalOutput").ap()

    x_d = din("x", [NT * 128, D])
    xs_d = din("xs", [NS, D])
    sret_d = din("sret", [4, H, 64, 64])
    ck_d = din("ck", [4, 2048, 512])
    cv_d = din("cv", [4, 2048, 512])
    sconv_d = din("sconv", [8, DFF])
    w_in_d = din("w_in", [D, 3584])
    w_out_d = din("w_out", [D, D])
    w_up_d = din("w_up", [D, 2 * DFF])
    w_down_d = din("w_down", [DFF, D])
    gnw_d = din("gnw_t", [128, 512])
    gnb_d = din("gnb_t", [128, 512])
    nwf_d = din("nwf_t", [128, D])
    nw1_d = din("nw1fm", [128, 8])
    nw2_d = din("nw2fm", [128, 8])
    convfm_d = din("convfm", [128, NCH, 4])
    vflag_d = din("vflag", [128, NT], BF16)
    cshapes = {
        "ident_bf": ([128, 128], BF16), "ident_f": ([128, 128], F32), "qdec": ([128, 8], F32),
        "kdec": ([128, 8], F32), "maskT": ([128, 128], F32), "cdec": ([128, 4, 64], F32),
        "cdec_s": ([128, 4, 64], F32), "qdec_s": ([128, 8], F32), "kdec_s": ([128, 8], F32),
        "maskT_s": ([128, 128], F32), "mask_n": ([128, 128], BF16), "colmask_s": ([128, 4, NS], BF16),
        "rowmask_s": ([128, 4], F32), "vflag_s": ([128, 1], BF16), "amask": ([128, WIN + 1, 128], BF16),
        "amask_s": ([128, 16, 8], BF16), "qaug": ([NT, 4, H, 128], BF16), "kaug": ([NT, 4, H, 128], BF16),
        "qaug_s": ([4, H, NS], BF16), "kaug_n": ([4, H, NS], BF16), "kaug_s": ([16, 4, H, 128], BF16),
        "kA_base": ([128, 128], BF16), "ecb": ([128, 16], F32),
    }
    cd = {k: din("c_" + k, shp, dt) for k, (shp, dt) in cshapes.items()}

    y_d = dout("y", [NT_MAIN * 128, D])
    ys_d = dout("ys", [NS, D])
    ret_d = dout("ret", [128, 4, 64])
    rets_d = dout("rets", [128, 4, 4, 64])
    NKV = NT - out_first_kv_tile
    wk_d = dout("wk", [NKV * 128, 512])
    wv_d = dout("wv", [NKV * 128, 512])
    wks_d = dout("wks", [4, 2048, 512])
    wvs_d = dout("wvs", [4, 2048, 512])
    convp_d = dout("convp", [2, DFF])
    convs_d = dout("convs", [8, DFF])

    win_b = nc.dram_tensor("win_b", [D, 3584], BF16).ap()
    wout_b = nc.dram_tensor("wout_b", [D, D], BF16).ap()
    wup_b = nc.dram_tensor("wup_b", [D, 2 * DFF], BF16).ap()
    wdn_b = nc.dram_tensor("wdn_b", [DFF, D], BF16).ap()

    def sb(name, shape, dt):
        return es.enter_context(nc.sbuf_tensor("s_" + name, list(shape), dt))

    def ps(name, shape, dt):
        return es.enter_context(nc.psum_tensor("p_" + name, list(shape), dt))

    Bk = [ps(f"bank{i}", [128, 512], F32) for i in range(6)]
    Bkb = [Buf(f"bank{i}") for i in range(6)]
    TR = [ps(f"trb{i}", [128, 1024], BF16) for i in range(2)]
    TRb = [Buf(f"trb{i}") for i in range(2)]
    gen_rot = [2, 3, 4, 5]
    gen_i = [0]
    tr_i = [0]

    def nextbank():
        i = gen_rot[gen_i[0] % 4]
        gen_i[0] += 1
        return Bk[i], Bkb[i]

    def nexttr():
        i = tr_i[0] % 2
        tr_i[0] += 1
        return TR[i], TRb[i]

    wring = [sb(f"wring{i}", [128, 4096], BF16) for i in range(NWS)]
    wringb = [Buf(f"wring{i}") for i in range(NWS)]
    KT = [sb(f"KT{i}", [128, H, 128], BF16) for i in range(NRING)]
    KTb = [Buf(f"KT{i}") for i in range(NRING)]
    KTab = [Buf(f"KTa{i}") for i in range(NRING)]
    VR = [sb(f"VR{i}", [128, H, 65], BF16) for i in range(NRING)]
    VRb = [Buf(f"VR{i}") for i in range(NRING)]
    xs_ = [sb(f"x{i}", [128, D], F32) for i in range(NXS)]
    xb = [Buf(f"x{i}") for i in range(NXS)]
    hT = [sb(f"hT{i}", [128, 8, 128], BF16) for i in range(NOS)]
    hTb = [Buf(f"hT{i}") for i in range(NOS)]
    NMAX = G * 128
    h2T = sb("h2T", [128, 8, NMAX], BF16)
    h2Tb = [Buf(f"h2T{g}") for g in range(G)]
    qk = [sb(f"qk{i}", [128, 1024], BF16) for i in range(NOS)]
    qkb = [Buf(f"qk{i}") for i in range(NOS)]
    qab = [Buf(f"qa{i}") for i in range(NOS)]
    kd = [sb(f"kd{i}", [128, 512], BF16) for i in range(NOS)]
    kdb = [Buf(f"kd{i}") for i in range(NOS)]
    vv = [sb(f"v{i}", [128, 512], BF16) for i in range(NOS)]
    vvb = [Buf(f"v{i}") for i in range(NOS)]
    sg = [sb(f"sg{i}", [128, 512], BF16) for i in range(NOS)]
    sgb = [Buf(f"sg{i}") for i in range(NOS)]
    gT = sb("gT", [128, NCH, NMAX], BF16)
    gTb = [Buf(f"gT{c}") for c in range(NCH)]
    gT_flat = gT[:, :, :].rearrange("p c n -> p (c n)")
    Kst = [gT_flat[:, i * 2048:(i + 1) * 2048].bitcast(F32).rearrange("p (a f) -> p a f", a=2) for i in range(2)]
    Vst = [gT_flat[:, (2 + i) * 2048:(3 + i) * 2048].bitcast(F32).rearrange("p (a f) -> p a f", a=2) for i in range(2)]
    Kstb = [Buf(f"Kst{i}") for i in range(2)]
    Vstb = [Buf(f"Vst{i}") for i in range(2)]
    S_kst = [T.new_dma_sem() for _ in range(2)]
    S_vst = [T.new_dma_sem() for _ in range(2)]
    xn = sb("xn", [128, D], BF16)
    xnb = Buf("xn")
    mix = sb("mix", [128, D], BF16)
    mixb = Buf("mix")
    tmpb16 = [sb(f"tmpb{i}", [128, 512], BF16) for i in range(G)]
    tmpb16b = [Buf(f"tmpb{i}") for i in range(G)]
    tmp_i = [0]
    Pt = [[sb(f"P{a}{b}", [128, 512], BF16) for b in range(2)] for a in range(2)]
    Ptb = [[Buf(f"P{a}{b}") for b in range(2)] for a in range(2)]
    xstg = sb("xstg", [128, D], F32)
    stage = [xstg[:, 0:512], xstg[:, 512:1024]]
    stageb = [Buf(f"stage{i}") for i in range(2)]
    stage_i = [0]
    osb = sb("osb", [128, 512], F32)
    osbb = Buf("osb")
    osq, osqb = stage[0], stageb[0]
    yb, ybb = stage[1], stageb[1]
    st = sb("st", [128, 64], F32)
    stb = Buf("st")
    uab = [sb(f"uab{i}", [128, NMAX + 2], F32) for i in range(2)]
    uabb = [Buf(f"uab{i}") for i in range(2)]
    cvt = [sb(f"cvt{i}", [128, NMAX], F32) for i in range(2)]
    cvtb = [Buf(f"cvt{i}") for i in range(2)]
    slu, slub = cvt, cvtb
    carry = sb("carry", [128, NCH, 2], F32)
    carryb = Buf("carry")
    convp_st = sb("convp_st", [128, 2, NCH], F32)
    convp_stb = Buf("convp_st")
    convs_st = sb("convs_st", [128, 8, NCH], F32)
    convs_stb = Buf("convs_st")
    sconvT = sb("sconvT", [128, 8, NCH], F32)
    sconvTb = Buf("sconvT")
    sc_tok = sb("sc_tok", [128, 2, 128], F32)
    sc_tokb = Buf("sc_tok")
    Sst = sb("Sst", [128, 4, 64], F32)
    Sstb = Buf("Sst")
    Sbf = sb("Sbf", [128, 4, 64], BF16)
    Sbfb = Buf("Sbf")
    Ss = sb("Ss", [128, 4, 4, 64], F32)
    Ssb = Buf("Ss")
    Ssbf = sb("Ssbf", [128, 4, 4, 64], BF16)
    Ssbfb = Buf("Ssbf")
    csb = {}
    csbb = Buf("consts")
    for k_ in ("ident_bf", "ident_f", "qdec", "kdec", "maskT", "cdec", "cdec_s", "qdec_s", "kdec_s",
               "maskT_s", "mask_n", "colmask_s", "rowmask_s", "vflag_s", "amask", "amask_s", "kA_base", "ecb"):
        shp, dt = cshapes[k_]
        csb[k_] = sb("k_" + k_, shp, dt)
    gnw = sb("gnw", [128, 512], F32)
    gnb = sb("gnb", [128, 512], F32)
    nwf = sb("nwf", [128, D], F32)
    nw1 = sb("nw1", [128, 8], F32)
    nw2 = sb("nw2", [128, 8], F32)
    convfm = sb("convfm", [128, NCH, 4], F32)
    vflag = sb("vflag", [128, NT], BF16)
    epsT = sb("epsT", [128, 1], F32)
    Kc = [sb(f"Kc{i}", [128, 512], BF16) for i in range(2)]
    Kcb = [Buf(f"Kc{i}") for i in range(2)]
    Vc = [sb(f"Vc{i}", [128, H, 65], BF16) for i in range(3)]
    Vcb = [Buf(f"Vc{i}") for i in range(3)]
    KTs = [sb(f"KTs{i}", [128, H, 128], BF16) for i in range(2)]
    KTsb = [Buf(f"KTs{i}") for i in range(2)]
    KTsab = [Buf(f"KTsa{i}") for i in range(2)]
    KTn = sb("KTn", [128, H, NS], BF16)
    KTnb = Buf("KTn")
    Vn = sb("Vn", [128, H, 65], BF16)
    Vnb = Buf("Vn")
    Pf = [sb(f"Pf{p}", [128, H, NS], BF16) for p in range(3)]
    Pfb = [Buf(f"Pf{p}") for p in range(3)]
    qm = sb("qm", [128, 4, 4, NS], BF16)
    qmb = Buf("qm")
    km, kmb = Kc[0], Kcb[0]
    so_tok = sb("so_tok", [128, 2, 128], F32)
    so_tokb = Buf("so_tok")

    S_const = T.new_dma_sem()
    S_wcast = [T.new_dma_sem() for _ in range(7)]
    S_stage = [T.new_dma_sem() for _ in range(2)]
    S_sotok = T.new_dma_sem()
    S_ss = T.new_dma_sem()
    S_sct = T.new_dma_sem()
    S_ktn = T.new_dma_sem()
    S_fin = T.new_dma_sem()
    S_w = [T.new_dma_sem() for _ in range(NWS)]
    S_x = [T.new_dma_sem() for _ in range(NXS)]
    S_y = [T.new_dma_sem() for _ in range(NXS)]
    S_qa = [T.new_dma_sem() for _ in range(NOS)]
    S_ka = [T.new_dma_sem() for _ in range(NRING)]
    S_out = T.new_dma_sem()
    S_kc = [T.new_dma_sem() for _ in range(2)]
    S_vc = [T.new_dma_sem() for _ in range(2)]
    S_ksa = [T.new_dma_sem() for _ in range(2)]

    def dve(fn, reads=(), writes=()):
        return T.op("dve", fn, reads, writes)

    def act(fn, reads=(), writes=()):
        return T.op("act", fn, reads, writes)

    def pool(fn, reads=(), writes=()):
        return T.op("pool", fn, reads, writes)

    def pe(fn, reads=(), writes=()):
        return T.op("pe", fn, reads, writes)

    def bc(ap, shape):
        return ap.to_broadcast(list(shape))

    wsc = {}

    const_pairs = [(csb[k_][:], cd[k_]) for k_ in csb]
    const_pairs += [(d_[:], s_) for d_, s_ in ((gnw, gnw_d), (gnb, gnb_d), (nwf, nwf_d), (nw1, nw1_d), (nw2, nw2_d),
                                                (convfm, convfm_d), (vflag, vflag_d))]
    T.dma("sp", lambda e: [e.dma_start(out=d_, in_=s_) for d_, s_ in const_pairs], S_const, writes=[csbb],
          n=len(const_pairs))

    def cast_group(gi, items):
        b_ = Buf(f"wcast{gi}")
        for key, _, _ in items:
            wsc[key] = b_
        T.dma("pool", lambda e: [e.dma_start(out=d_, in_=s_) for _, d_, s_ in items], S_wcast[gi], writes=[b_],
              n=len(items))

    for gi_, c in enumerate((1, 2, 5, 6)):
        cast_group(gi_, [(f"in{c}", win_b[:, c * 512:(c + 1) * 512], w_in_d[:, c * 512:(c + 1) * 512])])
    dve(lambda e: e.memset(epsT[:], EPS), writes=[csbb])
    dve(lambda e: e.memset(Sst[:], 0.0), writes=[Sstb])
    dve(lambda e: e.memset(Sbf[:], 0.0), writes=[Sbfb])
    dve(lambda e: e.memset(carry[:], 0.0), writes=[carryb])
    dve(lambda e: e.memset(h2T[:], 0.0), writes=h2Tb)
    for i in range(NRING):
        dve(lambda e, i=i: e.memset(KT[i][:], 0.0), writes=[KTb[i], KTab[i]])
        dve(lambda e, i=i: e.memset(VR[i][:], 0.0), writes=[VRb[i]])

    def late_setup():
        cast_group(4, [(f"in{c}", win_b[:, c * 512:(c + 1) * 512], w_in_d[:, c * 512:(c + 1) * 512]) for c in (0, 3, 4)]
                   + [(f"out{c}", wout_b[:, c * 512:(c + 1) * 512], w_out_d[:, c * 512:(c + 1) * 512]) for c in range(2)])
        items = []
        for s_ in range(6):
            c0, c1 = s_ * 512, min(DFF, s_ * 512 + 512)
            items.append((f"upA{s_}", wup_b[:, c0:c1], w_up_d[:, c0:c1]))
            items.append((f"upB{s_}", wup_b[:, DFF + c0:DFF + c1], w_up_d[:, DFF + c0:DFF + c1]))
        cast_group(5, items)
        items = []
        for s_ in range(6):
            r0, r1 = s_ * 512, min(DFF, s_ * 512 + 512)
            items.append((f"dn{s_}", wdn_b[r0:r1, :], w_down_d[r0:r1, :]))
        cast_group(6, items)

        for i in range(3):
            pool(lambda e, i=i: e.memset(Vc[i][:], 1.0), writes=[Vcb[i]])
        pool(lambda e: e.memset(Vn[:], 0.0), writes=[Vnb])
        pool(lambda e: e.memset(KTn[:], 0.0), writes=[KTnb])
        for i in range(2):
            pool(lambda e, i=i: e.memset(KTs[i][:], 0.0), writes=[KTsb[i], KTsab[i]])
        for i in range(NOS):
            pool(lambda e, i=i: e.memset(qk[i][:], 0.0), writes=[qkb[i], qab[i]])
        sret_v = sret_d.rearrange("b (pr two) d e -> two d b pr e", two=2)
        T.dma("sp", lambda e: [e.dma_start(out=Ss[par * 64:(par + 1) * 64, :, :, :], in_=sret_v[par]) for par in range(2)],
              S_ss, writes=[Ssb], n=2)
        dve(lambda e: e.tensor_copy(out=Ssbf[:], in_=Ss[:]), reads=[Ssb], writes=[Ssbfb])
        sconv_flat = sconv_d.rearrange("j (c f) -> (j c) f", f=128)
        NFL = 8 * NCH
        T.dma("sp", lambda e: [e.dma_start(out=sc_tok[:, 0, :], in_=sconv_flat[0:128, :]),
                               e.dma_start(out=sc_tok[0:NFL - 128, 1, :], in_=sconv_flat[128:NFL, :])],
              S_sct, writes=[sc_tokb], n=2)
        bk0, bk0b = nextbank()

        def f_sct(e):
            e.transpose(out=bk0[:, 0:128], in_=sc_tok[:, 0, :], identity=csb["ident_f"][:, :])
            return e.transpose(out=bk0[:, 128:NFL], in_=sc_tok[0:NFL - 128, 1, :],
                               identity=csb["ident_f"][0:NFL - 128, 0:NFL - 128])
        pe(f_sct, reads=[sc_tokb, csbb], writes=[bk0b])
        act(lambda e: e.copy(out=sconvT[:, :, :].rearrange("p j c -> p (j c)"), in_=bk0[:, 0:NFL]),
            reads=[bk0b], writes=[sconvTb])

    slab_list = []

    def slab_src(kind, idx):
        if kind == "in":
            return win_b[:, idx * 512:(idx + 1) * 512].rearrange("(k p) n -> p k n", p=128), [128, 8, 512]
        if kind == "out":
            return wout_b[:, idx * 512:(idx + 1) * 512].rearrange("(k p) n -> p k n", p=128), [128, 8, 512]
        if kind == "up":
            c0 = idx * 256
            return ([wup_b[:, c0:c0 + 256].rearrange("(k p) n -> p k n", p=128),
                     wup_b[:, DFF + c0:DFF + c0 + 256].rearrange("(k p) n -> p k n", p=128)], [128, 16, 256])
        if kind == "dn":
            r0, r1 = idx * 512, min(DFF, idx * 512 + 512)
            return wdn_b[r0:r1, :].rearrange("(c p) n -> p c n", p=128), [128, (r1 - r0) // 128, D]
        raise ValueError

    n_pre_groups = NT_PRE // G
    n_main_groups = (NT_MAIN + 1) // G
    for _ in range(n_pre_groups):
        for c in (1, 2, 5, 6):
            slab_list.append(("in", c))
    for _ in range(n_main_groups):
        for c in range(7):
            slab_list.append(("in", c))
        for c in range(2):
            slab_list.append(("out", c))
        for s_ in range(11):
            slab_list.append(("up", s_))
        for s_ in range(6):
            slab_list.append(("dn", s_))
    slab_next = [0]
    slab_cons = [0]

    def prefetch_slab():
        i = slab_next[0]
        if i >= len(slab_list):
            return
        slab_next[0] += 1
        kind, idx = slab_list[i]
        src, shp = slab_src(kind, idx)
        slot = i % NWS
        n_el = shp[1] * shp[2]
        dst = wring[slot][:, 0:n_el].rearrange("p (a b) -> p a b", a=shp[1])
        if kind == "up":
            T.dma("sp", lambda e, d=dst, s=src: [e.dma_start(out=d[:, 0:8, :], in_=s[0]),
                                                  e.dma_start(out=d[:, 8:16, :], in_=s[1])], S_w[slot],
                  reads=[wsc[f"upA{idx // 2}"], wsc[f"upB{idx // 2}"]], writes=[wringb[slot]], n=2)
        else:
            T.dma("sp", lambda e, d=dst, s=src: e.dma_start(out=d, in_=s), S_w[slot],
                  reads=[wsc[f"{kind}{idx}"]], writes=[wringb[slot]])

    def take_slab(kind, idx):
        i = slab_cons[0]
        assert slab_list[i] == (kind, idx), (slab_list[i], kind, idx)
        slab_cons[0] += 1
        slot = i % NWS
        _, shp = slab_src(kind, idx)
        n_el = shp[1] * shp[2]
        view = wring[slot][:, 0:n_el].rearrange("p (a b) -> p a b", a=shp[1])
        return view, wringb[slot]

    for _ in range(NWS):
        prefetch_slab()

    class Tile:
        pass

    def load_x(tl):
        s = tl.xslot
        if tl.kind == "S":
            src = xs_d[:, :]
        else:
            src = x_d[tl.T * 128:(tl.T + 1) * 128, :]
        T.dma("act", lambda e, s=s, src=src, nr=tl.nr: e.dma_start(out=xs_[s][:nr, :], in_=src), S_x[s],
              writes=[xb[s]])

    S_xstg = T.new_dma_sem()

    def load_x_stage(tl):
        src = xs_d[:, :] if tl.kind == "S" else x_d[tl.T * 128:(tl.T + 1) * 128, :]
        T.dma("act", lambda e, src=src, nr=tl.nr: e.dma_start(out=xstg[:nr, :], in_=src), S_xstg,
              writes=[stageb[0], stageb[1]])

    def prep_tile(tl):
        load_x_stage(tl)
        prep_norm(tl)

    def prep_norm(tl, phase="all"):
        rmsnorm_T(tl, xstg, [stageb[0], stageb[1]], nw1, lambda nr, tl=tl: tl.hbuf[:, :, :nr], tl.hbufb, phase)

    def prep_group(grp):
        for tl in grp[1]:
            prep_tile(tl)

    def rmsnorm_T(tl, xsrc, xsrcb, nwfm, dst_fn, dstb, phase="all"):
        nr = tl.nr
        if phase in ("all", "a"):
            rmsnorm_a(nr, xsrc, xsrcb)
        if phase in ("all", "b"):
            rmsnorm_b(nr, nwfm, dst_fn, dstb)

    def rmsnorm_a(nr, xsrc, xsrcb):
        act(lambda e: e.activation(out=xn[:nr, :], in_=xsrc[:nr, :], func=AF.Square, accum_out=st[:nr, 0:1]),
            reads=list(xsrcb), writes=[xnb, stb])
        act(lambda e: e.activation(out=st[:nr, 1:2], in_=st[:nr, 0:1], func=AF.Sqrt, scale=1.0 / D,
                                   bias=epsT[:nr, :]), reads=[stb, csbb], writes=[stb])
        dve(lambda e: e.reciprocal(out=st[:nr, 2:3], in_=st[:nr, 1:2]), reads=[stb], writes=[stb])
        dve(lambda e: e.tensor_scalar(out=xn[:nr, :], in0=xsrc[:nr, :], scalar1=st[:nr, 2:3], scalar2=None,
                                      op0=ALU.mult), reads=list(xsrcb) + [stb], writes=[xnb])

    def rmsnorm_b(nr, nwfm, dst_fn, dstb):
        tr, trb = nexttr()

        def f(e):
            last = None
            for k in range(8):
                last = e.transpose(out=tr[:, k * 128:k * 128 + nr], in_=xn[:nr, k * 128:(k + 1) * 128],
                                   identity=csb["ident_bf"][:nr, :nr])
            return last
        pe(f, reads=[xnb, csbb], writes=[trb])
        trv = tr[:].rearrange("p (k t) -> p k t", k=8)[:, :, :nr]
        dve(lambda e: e.tensor_tensor(out=dst_fn(nr), in0=trv, in1=bc(nwfm[:].unsqueeze(2), [128, 8, nr]),
                                      op=ALU.mult), reads=[trb, csbb], writes=[dstb])

    def proj_mm(tl, wview, wb, outbank, outbankb, lhs, lhsb, ncols=512):
        nr = tl.nr

        def f(e):
            last = None
            for k in range(8):
                last = e.matmul(outbank[:nr, :ncols], lhsT=lhs[:, k, :nr], rhs=wview[:, k, :ncols],
                                start=(k == 0), stop=(k == 7))
            return last
        pe(f, reads=[lhsb, wb], writes=[outbankb])

    def ring(Tt):
        return Tt % NRING

    def evac_slab(tl, c, bk, bkb, full):
        nr, o = tl.nr, tl.oslot
        S_tile = tl.kind == "S"
        if c == 0:
            ti = tl.g
            qd_ = csb["qdec_s"] if S_tile else csb["qdec"]
            dve(lambda e: e.tensor_tensor(out=tmpb16[ti][:nr, :].rearrange("p (h d) -> p h d", h=8),
                                          in0=bk[:nr, :].rearrange("p (h d) -> p h d", h=8),
                                          in1=bc(qd_[:nr, :].unsqueeze(2), [nr, 8, 64]), op=ALU.mult),
                reads=[bkb, csbb], writes=[tmpb16b[ti]])
            tl.post.append(("tr4", ti, 0))
        elif c == 1:
            kd_ = csb["kdec_s"] if S_tile else csb["kdec"]
            dve(lambda e: e.tensor_tensor(out=kd[o][:nr, :].rearrange("p (h d) -> p h d", h=8),
                                          in0=bk[:nr, :].rearrange("p (h d) -> p h d", h=8),
                                          in1=bc(kd_[:nr, :].unsqueeze(2), [nr, 8, 64]), op=ALU.mult),
                reads=[bkb, csbb], writes=[kdb[o]])
            if full:
                tl.post.append(("tr4k", None, 512))
        elif c == 2:
            act(lambda e: e.copy(out=vv[o][:nr, :], in_=bk[:nr, :]), reads=[bkb], writes=[vvb[o]])
        elif c == 3:
            act(lambda e: e.activation(out=sg[o][:nr, :], in_=bk[:nr, :], func=AF.Silu), reads=[bkb],
                writes=[sgb[o]])
        elif c == 4:
            ti = tl.g
            act(lambda e: e.activation(out=tmpb16[ti][:nr, :], in_=bk[:nr, :], func=AF.Copy, scale=0.125),
                reads=[bkb], writes=[tmpb16b[ti]])
            tl.post.append(("tr8q", ti, None))
        elif c == 5:
            si = stage_i[0] % 2
            stage_i[0] += 1
            act(lambda e: e.copy(out=stage[si][:nr, :], in_=bk[:nr, :]), reads=[bkb], writes=[stageb[si]])
            out_kv(tl, si, wk_d, wks_d)
            ti = tl.g
            dve(lambda e: e.tensor_copy(out=tmpb16[ti][:nr, :], in_=stage[si][:nr, :]), reads=[stageb[si]],
                writes=[tmpb16b[ti]])
            tl.post.append(("tr8k", ti, None))
        elif c == 6:
            si = stage_i[0] % 2
            stage_i[0] += 1
            act(lambda e: e.copy(out=stage[si][:nr, :], in_=bk[:nr, :]), reads=[bkb], writes=[stageb[si]])
            out_kv(tl, si, wv_d, wvs_d)
            if S_tile:
                dve(lambda e: e.tensor_copy(out=Vn[:nr, :, 0:64],
                                            in_=stage[si][:nr, :].rearrange("p (h d) -> p h d", h=8)),
                    reads=[stageb[si]], writes=[Vnb])
                dve(lambda e: e.tensor_copy(out=Vn[:nr, :, 64:65],
                                            in_=bc(csb["vflag_s"][:nr, 0:1].unsqueeze(2), [nr, 8, 1])),
                    reads=[csbb], writes=[Vnb])
            else:
                rs = ring(tl.T)
                dve(lambda e: e.tensor_copy(out=VR[rs][:nr, :, 0:64],
                                            in_=stage[si][:nr, :].rearrange("p (h d) -> p h d", h=8)),
                    reads=[stageb[si]], writes=[VRb[rs]])
                dve(lambda e: e.tensor_copy(out=VR[rs][:nr, :, 64:65],
                                            in_=bc(vflag[:nr, tl.T:tl.T + 1].unsqueeze(2), [nr, 8, 1])),
                    reads=[csbb], writes=[VRb[rs]])

    def out_kv(tl, si, dstP, dstS):
        nr = tl.nr
        if tl.kind == "S":
            def f(e):
                r = []
                for b in range(4):
                    r.append(e.dma_start(out=dstS[b, 2040:2048, :], in_=stage[si][b * 10 + 2:b * 10 + 10, :]))
                return r
            T.dma("pool", f, S_stage[si], reads=[stageb[si]], n=4, is_output=True)
        elif tl.T >= out_first_kv_tile:
            r0 = (tl.T - out_first_kv_tile) * 128
            T.dma("pool", lambda e: e.dma_start(out=dstP[r0:r0 + 128, :], in_=stage[si][:nr, :]), S_stage[si],
                  reads=[stageb[si]], is_output=True)

    def run_post(tl):
        nr, o = tl.nr, tl.oslot
        for kind, ti, _ in tl.post:
            if kind in ("tr4", "tr4k"):
                src = tmpb16[ti] if kind == "tr4" else kd[o]
                srcb = tmpb16b[ti] if kind == "tr4" else kdb[o]
                off = 0 if kind == "tr4" else 512
                tr, trb = nexttr()

                def f(e, src=src, tr=tr):
                    last = None
                    for pr in range(4):
                        last = e.transpose(out=tr[:, pr * 128:pr * 128 + nr], in_=src[:nr, pr * 128:(pr + 1) * 128],
                                           identity=csb["ident_bf"][:nr, :nr])
                    return last
                pe(f, reads=[srcb, csbb], writes=[trb])
                act(lambda e, tr=tr, off=off: e.copy(
                    out=qk[o][:, off:off + 512].rearrange("p (a t) -> p a t", a=4)[:, :, :nr],
                    in_=tr[:, 0:512].rearrange("p (a t) -> p a t", a=4)[:, :, :nr]),
                    reads=[trb], writes=[qkb[o]])
            elif kind in ("tr8q", "tr8k"):
                tr, trb = nexttr()

                def f(e, ti=ti, tr=tr):
                    last = None
                    for h in range(8):
                        last = e.transpose(out=tr[0:64, h * 128:h * 128 + nr], in_=tmpb16[ti][:nr, h * 64:(h + 1) * 64],
                                           identity=csb["ident_bf"][:nr, :nr])
                    return last
                pe(f, reads=[tmpb16b[ti], csbb], writes=[trb])
                trv = tr[0:64, :].rearrange("p (h t) -> p h t", h=8)[:, :, :nr]
                if kind == "tr8q":
                    QTv = qk[o][0:68, :].rearrange("p (h t) -> p h t", h=8)
                    dve(lambda e, trv=trv, QTv=QTv: e.tensor_copy(out=QTv[0:64, :, :nr], in_=trv),
                        reads=[trb], writes=[qkb[o]])
                    if tl.kind == "S":
                        T.dma("pool", lambda e, QTv=QTv: e.dma_start(out=QTv[64:68, :, :NS], in_=cd["qaug_s"]),
                              S_qa[o], reads=[qkb[o]], writes=[qab[o]])
                    else:
                        T.dma("pool", lambda e, QTv=QTv: e.dma_start(out=QTv[64:68, :, :], in_=cd["qaug"][tl.T]),
                              S_qa[o], reads=[qkb[o]], writes=[qab[o]])
                else:
                    if tl.kind == "S":
                        act(lambda e, trv=trv: e.copy(out=KTn[0:64, :, :nr], in_=trv), reads=[trb], writes=[KTnb])
                        T.dma("pool", lambda e: e.dma_start(out=KTn[64:68, :, :], in_=cd["kaug_n"]), S_ktn,
                              writes=[KTnb])
                    else:
                        rs = ring(tl.T)
                        act(lambda e, trv=trv, rs=rs: e.copy(out=KT[rs][0:64, :, :nr], in_=trv), reads=[trb],
                            writes=[KTb[rs]])
                        T.dma("pool", lambda e, rs=rs: e.dma_start(out=KT[rs][64:68, :, :], in_=cd["kaug"][tl.T]),
                              S_ka[rs], reads=[KTb[rs]], writes=[KTab[rs]])
        tl.post = []

    def retention(tl):
        nr, o = tl.nr, tl.oslot
        S_tile = tl.kind == "S"
        qdT = qk[o][:, 0:512].rearrange("p (a t) -> p a t", a=4)
        kdT = qk[o][:, 512:1024].rearrange("p (a t) -> p a t", a=4)
        mk = csb["maskT_s"] if S_tile else csb["maskT"]

        def f(e):
            last = None
            for h in range(8):
                hp, pr = h % 2, h // 2
                last = e.matmul(Bk[hp][:nr, pr * 128:pr * 128 + nr],
                                lhsT=kdT[hp * 64:(hp + 1) * 64, pr, :nr], rhs=qdT[hp * 64:(hp + 1) * 64, pr, :nr],
                                start=True, stop=True)
            return last
        pe(f, reads=[qkb[o]], writes=[Bkb[0], Bkb[1]])
        for hb in range(2):
            if S_tile:
                dve(lambda e, hb=hb: e.memset(Pt[1][hb][:, :], 0.0), writes=[Ptb[1][hb]])
            dve(lambda e, hb=hb: e.tensor_tensor(
                out=Pt[1][hb][:nr, :].rearrange("p (a t) -> p a t", a=4)[:, :, :nr],
                in0=Bk[hb][:nr, :].rearrange("p (a t) -> p a t", a=4)[:, :, :nr],
                in1=bc(mk[:nr, :nr].unsqueeze(1), [nr, 4, nr]), op=ALU.mult),
                reads=[Bkb[hb], csbb], writes=[Ptb[1][hb]])
        if STAGE < 2.31:
            return
        if S_tile:
            for b in range(4):
                dve(lambda e, b=b: e.tensor_tensor(out=qm[:, b, :, :], in0=qdT[:, :, :NS],
                                                   in1=bc(csb["colmask_s"][:, b, :].unsqueeze(1), [128, 4, NS]),
                                                   op=ALU.mult), reads=[qkb[o], csbb], writes=[qmb])
        bo2 = [nextbank(), nextbank()]

        def f2(e):
            last = None
            for hp in range(2):
                bo = bo2[hp][0]
                for pr in range(4):
                    h = 2 * pr + hp
                    e.matmul(bo[:nr, pr * 64:(pr + 1) * 64], lhsT=Pt[1][hp][:, pr * 128:pr * 128 + nr],
                             rhs=vv[o][:, h * 64:(h + 1) * 64], start=True, stop=False, skip_group_check=True)
                    if S_tile:
                        for b in range(4):
                            last = e.matmul(bo[:nr, pr * 64:(pr + 1) * 64], lhsT=qm[hp * 64:(hp + 1) * 64, b, pr, :nr],
                                            rhs=Ssbf[hp * 64:(hp + 1) * 64, b, pr, :], start=False, stop=(b == 3),
                                            skip_group_check=True)
                    else:
                        last = e.matmul(bo[:nr, pr * 64:(pr + 1) * 64], lhsT=qdT[hp * 64:(hp + 1) * 64, pr, :nr],
                                        rhs=Sbf[hp * 64:(hp + 1) * 64, pr, :], start=False, stop=True,
                                        skip_group_check=True)
            return last
        pe(f2, reads=[Ptb[1][0], Ptb[1][1], vvb[o], qkb[o], qmb, Sbfb, Ssbfb], writes=[bo2[0][1], bo2[1][1]])
        if STAGE < 2.32:
            return
        for hp in range(2):
            act(lambda e, hp=hp: e.copy(out=osb[:nr, :].rearrange("p (a w d) -> p a w d", a=4, w=2)[:, :, hp, :],
                                        in_=bo2[hp][0][:nr, 0:256].rearrange("p (a d) -> p a d", a=4)),
                reads=[bo2[hp][1]], writes=[osbb])
        act(lambda e: e.activation(out=osq[:nr, :], in_=osb[:nr, :], func=AF.Square), reads=[osbb], writes=[osqb])
        dve(lambda e: e.tensor_reduce(out=st[:nr, 8:16], in_=osb[:nr, :].rearrange("p (h d) -> p h d", h=8),
                                      axis=AX.X, op=ALU.add), reads=[osbb], writes=[stb])
        dve(lambda e: e.tensor_reduce(out=st[:nr, 16:24], in_=osq[:nr, :].rearrange("p (h d) -> p h d", h=8),
                                      axis=AX.X, op=ALU.add), reads=[osqb], writes=[stb])
        dve(lambda e: e.tensor_scalar(out=st[:nr, 24:32], in0=st[:nr, 8:16], scalar1=1.0 / 64, scalar2=None,
                                      op0=ALU.mult), reads=[stb], writes=[stb])
        dve(lambda e: e.tensor_tensor(out=st[:nr, 32:40], in0=st[:nr, 24:32], in1=st[:nr, 24:32], op=ALU.mult),
            reads=[stb], writes=[stb])
        dve(lambda e: e.scalar_tensor_tensor(out=st[:nr, 40:48], in0=st[:nr, 16:24], scalar=1.0 / 64,
                                             in1=st[:nr, 32:40], op0=ALU.mult, op1=ALU.subtract),
            reads=[stb], writes=[stb])
        act(lambda e: e.activation(out=st[:nr, 48:56], in_=st[:nr, 40:48], func=AF.Sqrt, bias=epsT[:nr, :]),
            reads=[stb, csbb], writes=[stb])
        dve(lambda e: e.reciprocal(out=st[:nr, 56:64], in_=st[:nr, 48:56]), reads=[stb], writes=[stb])
        dve(lambda e: e.tensor_tensor(out=yb[:nr, :].rearrange("p (h d) -> p h d", h=8),
                                      in0=osb[:nr, :].rearrange("p (h d) -> p h d", h=8),
                                      in1=bc(st[:nr, 24:32].unsqueeze(2), [nr, 8, 64]), op=ALU.subtract),
            reads=[osbb, stb], writes=[ybb])
        dve(lambda e: e.tensor_tensor(out=yb[:nr, :].rearrange("p (h d) -> p h d", h=8),
                                      in0=yb[:nr, :].rearrange("p (h d) -> p h d", h=8),
                                      in1=bc(st[:nr, 56:64].unsqueeze(2), [nr, 8, 64]), op=ALU.mult),
            reads=[ybb, stb], writes=[ybb])
        if STAGE < 2.33:
            return
        pool(lambda e: e.tensor_tensor(out=yb[:nr, :], in0=yb[:nr, :], in1=gnw[:nr, :], op=ALU.mult),
             reads=[ybb, csbb], writes=[ybb])
        pool(lambda e: e.tensor_tensor(out=yb[:nr, :], in0=yb[:nr, :], in1=gnb[:nr, :], op=ALU.add),
             reads=[ybb, csbb], writes=[ybb])
        pool(lambda e: e.tensor_tensor(out=sg[o][:nr, :], in0=yb[:nr, :], in1=sg[o][:nr, :], op=ALU.mult),
             reads=[ybb, sgb[o]], writes=[sgb[o]])
        if STAGE < 2.34:
            return
        if not S_tile:
            state_update(kd[o], kdb[o], vv[o], vvb[o], nr, Sst, Sstb, Sbf, Sbfb, None, csb["cdec"])
        else:
            for b in range(4):
                dve(lambda e, b=b: e.tensor_scalar(out=km[:nr, :], in0=kd[o][:nr, :],
                                                   scalar1=csb["rowmask_s"][:nr, b:b + 1], scalar2=None,
                                                   op0=ALU.mult), reads=[kdb[o], csbb], writes=[kmb])
                state_update(km, kmb, vv[o], vvb[o], nr, Ss, Ssb, Ssbf, Ssbfb, b, csb["cdec_s"])

    def state_update(kdt, kdtb, vt, vtb, nr, S_, S_b, Sb_, Sb_b, b, cdec_):
        bd, bdb = nextbank()

        def f(e):
            last = None
            for pr in range(4):
                for w in range(2):
                    last = e.matmul(bd[:, pr * 128 + w * 64:pr * 128 + w * 64 + 64],
                                    lhsT=kdt[:nr, pr * 128:(pr + 1) * 128],
                                    rhs=vt[:nr, (2 * pr + w) * 64:(2 * pr + w + 1) * 64], start=True, stop=True)
            return last
        pe(f, reads=[kdtb, vtb], writes=[bdb])
        bdv = bd[:, :].rearrange("p (a w e) -> p a w e", a=4, w=2)
        if b is None:
            Sv = lambda lo, hi: S_[lo:hi, :, :]
            Sbv = Sb_[:, :, :]
            Sall = S_[:, :, :]
        else:
            Sv = lambda lo, hi: S_[lo:hi, b, :, :]
            Sbv = Sb_[:, b, :, :]
            Sall = S_[:, b, :, :]
        for w in range(2):
            dve(lambda e, w=w: e.tensor_tensor(out=Sv(w * 64, w * 64 + 64), in0=Sv(w * 64, w * 64 + 64),
                                               in1=bdv[w * 64:(w + 1) * 64, :, w, :], op=ALU.add),
                reads=[bdb, S_b], writes=[S_b])
        dve(lambda e: e.tensor_tensor(out=Sall, in0=Sall, in1=cdec_[:, :, :], op=ALU.mult),
            reads=[S_b, csbb], writes=[S_b])
        dve(lambda e: e.tensor_copy(out=Sbv, in_=Sall), reads=[S_b], writes=[Sb_b])

    def pv_mm(e, bank, Pget, Vget, nr, first, last_blk, heads):
        last = None
        for h in heads:
            last = e.matmul(bank[:nr, (h % 4) * 65:(h % 4) * 65 + 65], lhsT=Pget(h), rhs=Vget(h),
                            start=(first and (h % 4) == 0), stop=(last_blk and (h % 4) == 3),
                            skip_group_check=True)
        return last

    def attention_P(tl):
        nr, o = tl.nr, tl.oslot
        QTv = qk[o][:, :].rearrange("p (h t) -> p h t", h=8)
        blocks = list(range(0, min(WIN, tl.T) + 1))
        first = [True, True]
        pend = None
        for bi, j in enumerate(blocks):
            rs = ring(tl.T - j)
            par = bi % 2
            for hb in range(2):
                def f(e, hb=hb, rs=rs):
                    last = None
                    for h in range(hb * 4, hb * 4 + 4):
                        last = e.matmul(Bk[hb][:, (h % 4) * 128:(h % 4) * 128 + nr], lhsT=KT[rs][:, h, :],
                                        rhs=QTv[:, h, :nr], start=True, stop=True)
                    return last
                pe(f, reads=[KTb[rs], KTab[rs], qkb[o], qab[o]], writes=[Bkb[hb]])
            if pend is not None:
                pend()
            for hb in range(2):
                act(lambda e, hb=hb, par=par: e.activation(out=Pt[par][hb][:, :], in_=Bk[hb][:, :], func=AF.Exp),
                    reads=[Bkb[hb]], writes=[Ptb[par][hb]])
                dve(lambda e, hb=hb, par=par, j=j: e.tensor_tensor(
                    out=Pt[par][hb][:, :].rearrange("p (a t) -> p a t", a=4),
                    in0=Pt[par][hb][:, :].rearrange("p (a t) -> p a t", a=4),
                    in1=bc(csb["amask"][:, j, :].unsqueeze(1), [128, 4, 128]), op=ALU.mult),
                    reads=[Ptb[par][hb], csbb], writes=[Ptb[par][hb]])
            lastb = bi == len(blocks) - 1

            def mk(par=par, rs=rs, lastb=lastb):
                def go():
                    for hb in range(2):
                        fst = first[hb]
                        first[hb] = False
                        pe(lambda e, hb=hb, fst=fst: pv_mm(e, Bk[2 + hb],
                                                           lambda h: Pt[par][hb][:, (h % 4) * 128:(h % 4) * 128 + nr],
                                                           lambda h: VR[rs][:, h, :], nr, fst, lastb,
                                                           range(hb * 4, hb * 4 + 4)),
                           reads=[Ptb[par][hb], VRb[rs]], writes=[Bkb[2 + hb]])
                return go
            pend = mk()
            yield
        pend()
        attn_finish(nr, 2)
        mix_transpose(tl)
        yield

    def attn_finish(nr, b0):
        for hb in range(2):
            av_ = Bk[b0 + hb][:nr, 0:260].rearrange("p (a e) -> p a e", a=4)
            dve(lambda e, av_=av_, hb=hb: e.tensor_scalar(out=st[:nr, hb * 4:hb * 4 + 4].unsqueeze(2),
                                                          in0=av_[:, :, 64:65], scalar1=1e-30, scalar2=None,
                                                          op0=ALU.max), reads=[Bkb[b0 + hb]], writes=[stb])
            dve(lambda e, hb=hb: e.reciprocal(out=st[:nr, hb * 4:hb * 4 + 4], in_=st[:nr, hb * 4:hb * 4 + 4]),
                reads=[stb], writes=[stb])
            dve(lambda e, av_=av_, hb=hb: e.tensor_tensor(
                out=mix[:nr, 512 + hb * 256:512 + hb * 256 + 256].rearrange("p (a d) -> p a d", a=4),
                in0=av_[:, :, 0:64], in1=bc(st[:nr, hb * 4:hb * 4 + 4].unsqueeze(2), [nr, 4, 64]), op=ALU.mult),
                reads=[Bkb[b0 + hb], stb], writes=[mixb])

    def attention_S(tl):
        nr, o = tl.nr, tl.oslot
        QTv = qk[o][:, :].rearrange("p (h t) -> p h t", h=8)
        first = [True, True]

        def load_pair(b, c2, sp_, gdep=False):
            extra = gTb if gdep else []
            T.dma("sp", lambda e: e.dma_start(
                out=Kst[sp_], in_=ck_d[b, c2 * 256:(c2 + 1) * 256, :].rearrange("(a p) f -> p a f", p=128)),
                S_kst[sp_], writes=[Kstb[sp_]] + extra)
            T.dma("sp", lambda e: e.dma_start(
                out=Vst[sp_], in_=cv_d[b, c2 * 256:(c2 + 1) * 256, :].rearrange("(a p) f -> p a f", p=128)),
                S_vst[sp_], writes=[Vstb[sp_]] + extra)

        seq = [(b, ch) for b in range(4) for ch in range(16)]
        pairs = [(b, c2) for b in range(4) for c2 in range(8)]
        NB_ = len(seq)
        trs = {}

        def stage_K(i):
            sp_, blk, cp = (i // 2) % 2, i % 2, i % 2
            extra = gTb if i >= NB_ - 2 else []
            dve(lambda e: e.tensor_copy(out=Kc[cp][:, :], in_=Kst[sp_][:, blk, :]), reads=[Kstb[sp_]] + extra,
                writes=[Kcb[cp]])

        def stage_V(i):
            sp_, blk, vp = (i // 2) % 2, i % 2, i % 3
            extra = gTb if i >= NB_ - 2 else []
            act(lambda e: e.copy(out=Vc[vp][:, :, 0:64], in_=Vst[sp_][:, blk, :].rearrange("p (h d) -> p h d", h=8)),
                reads=[Vstb[sp_]] + extra, writes=[Vcb[vp]])

        def stage_T(i):
            cp = i % 2
            tr, trb = nexttr()
            trs[i] = (tr, trb)

            def f(e):
                last = None
                for h in range(8):
                    last = e.transpose(out=tr[0:64, h * 128:(h + 1) * 128], in_=Kc[cp][:, h * 64:(h + 1) * 64],
                                       identity=csb["ident_bf"][:, :])
                return last
            pe(f, reads=[Kcb[cp], csbb], writes=[trb])

        def stage_KT(i):
            b, ch = seq[i]
            kp = i % 2
            tr, trb = trs.pop(i)
            dve(lambda e: e.tensor_copy(out=KTs[kp][0:64, :, :], in_=tr[0:64, :].rearrange("p (h t) -> p h t", h=8)),
                reads=[trb], writes=[KTsb[kp]])
            dve(lambda e: e.tensor_scalar(out=KTs[kp][64:68, :, :],
                                          in0=bc(csb["kA_base"][64:68, :].unsqueeze(1), [4, 8, 128]),
                                          scalar1=csb["ecb"][64:68, ch:ch + 1], scalar2=None, op0=ALU.add),
                reads=[csbb], writes=[KTsab[kp]])

        def stage_B(i):
            b, ch = seq[i]
            kp, pp, q0 = i % 2, i % 3, b * 10 + 2
            if ch == 0 and i > 0:
                pass
            for hb in range(2):
                def f2(e, hb=hb):
                    last = None
                    for h in range(hb * 4, hb * 4 + 4):
                        last = e.matmul(Bk[hb][:, (h % 4) * 128:(h % 4) * 128 + 8], lhsT=KTs[kp][:, h, :],
                                        rhs=QTv[:, h, q0:q0 + 8], start=True, stop=True)
                    return last
                pe(f2, reads=[KTsb[kp], KTsab[kp], qkb[o], qab[o]], writes=[Bkb[hb]])
            if ch < 3:
                pool(lambda e: e.memset(Pf[pp][:], 0.0), writes=[Pfb[pp]])
            for hb in range(2):
                act(lambda e, hb=hb: e.activation(
                    out=Pf[pp][:, hb * 4:hb * 4 + 4, q0:q0 + 8],
                    in_=Bk[hb][:, :].rearrange("p (a t) -> p a t", a=4)[:, :, 0:8], func=AF.Exp),
                    reads=[Bkb[hb]], writes=[Pfb[pp]])
            dve(lambda e: e.tensor_tensor(
                out=Pf[pp][:, :, q0:q0 + 8], in0=Pf[pp][:, :, q0:q0 + 8],
                in1=bc(csb["amask_s"][:, ch, :].unsqueeze(1), [128, 8, 8]), op=ALU.mult),
                reads=[Pfb[pp], csbb], writes=[Pfb[pp]])

        def stage_C(i):
            pp, vp = i % 3, i % 3
            for hb in range(2):
                fst = first[hb]
                first[hb] = False
                pe(lambda e, hb=hb, fst=fst: pv_mm(
                    e, Bk[4 + hb], lambda h: Pf[pp][:, h, :NS], lambda h: Vc[vp][:, h, :], NS, fst, False,
                    range(hb * 4, hb * 4 + 4)),
                    reads=[Pfb[pp], Vcb[vp]], writes=[Bkb[4 + hb]])

        load_pair(pairs[0][0], pairs[0][1], 0, gdep=True)
        load_pair(pairs[1][0], pairs[1][1], 1)
        def refill(i):
            if i % 2 == 1 and i // 2 + 2 < len(pairs):
                pb, pc2 = pairs[i // 2 + 2]
                load_pair(pb, pc2, (i // 2) % 2)

        for step in range(-3, NB_ + 1):
            if 0 <= step - 1 < NB_:
                stage_C(step - 1)
            if 0 <= step < NB_:
                stage_B(step)
            if 0 <= step + 3 < NB_:
                stage_K(step + 3)
            if 0 <= step + 2 < NB_:
                stage_V(step + 2)
                refill(step + 2)
                stage_T(step + 2)
                stage_KT(step + 2)
            if step >= 0:
                yield
        for hb in range(2):
            def f3(e, hb=hb):
                last = None
                for h in range(hb * 4, hb * 4 + 4):
                    last = e.matmul(Bk[hb][:NS, (h % 4) * 128:(h % 4) * 128 + NS], lhsT=KTn[:, h, :NS],
                                    rhs=QTv[:, h, :NS], start=True, stop=True)
                return last
            pe(f3, reads=[KTnb, qkb[o], qab[o]], writes=[Bkb[hb]])
        for hb in range(2):
            act(lambda e, hb=hb: e.activation(
                out=Pf[0][:NS, hb * 4:hb * 4 + 4, :NS],
                in_=Bk[hb][:NS, :].rearrange("p (a t) -> p a t", a=4)[:, :, :NS], func=AF.Exp),
                reads=[Bkb[hb]], writes=[Pfb[0]])
        dve(lambda e: e.tensor_tensor(
            out=Pf[0][:NS, :, :NS], in0=Pf[0][:NS, :, :NS],
            in1=bc(csb["mask_n"][:NS, :NS].unsqueeze(1), [NS, 8, NS]), op=ALU.mult),
            reads=[Pfb[0], csbb], writes=[Pfb[0]])
        for hb in range(2):
            fst = first[hb]
            first[hb] = False
            pe(lambda e, hb=hb, fst=fst: pv_mm(e, Bk[4 + hb], lambda h: Pf[0][:NS, h, :NS],
                                               lambda h: Vn[:NS, h, :], NS, fst, True, range(hb * 4, hb * 4 + 4)),
               reads=[Pfb[0], Vnb], writes=[Bkb[4 + hb]])
        yield "done"

    def mix_transpose(tl):
        nr, o = tl.nr, tl.oslot
        tr, trb = nexttr()

        def f(e):
            last = None
            for k in range(8):
                src = sg[o][:nr, k * 128:(k + 1) * 128] if k < 4 else mix[:nr, k * 128:(k + 1) * 128]
                last = e.transpose(out=tr[:, k * 128:k * 128 + nr], in_=src, identity=csb["ident_bf"][:nr, :nr])
            return last
        pe(f, reads=[mixb, sgb[o], csbb], writes=[trb])
        act(lambda e: e.copy(out=hT[o][:, :, :nr], in_=tr[:].rearrange("p (k t) -> p k t", k=8)[:, :, :nr]),
            reads=[trb], writes=[hTb[o]])

    tile_ctr = [0]

    def new_tile(kind, Tt):
        tl = Tile()
        tl.kind = kind
        tl.T = Tt
        tl.nr = NS if kind == "S" else 128
        i = tile_ctr[0]
        tile_ctr[0] += 1
        tl.xslot = i % NXS
        tl.oslot = i % NOS
        tl.post = []
        tl.hbuf = hT[tl.oslot]
        tl.hbufb = hTb[tl.oslot]
        return tl

    all_groups = []
    for gi in range(n_pre_groups):
        tl_list = [new_tile("P", gi * G + g) for g in range(G)]
        if (n_pre_groups - gi) % 2 == 1:
            for g_, tl_ in enumerate(tl_list):
                tl_.hbuf = h2T[:, :, g_ * 128:(g_ + 1) * 128]
                tl_.hbufb = h2Tb[g_]
        all_groups.append(("pre", tl_list))
    main_tiles = [("P", NT_PRE + m) for m in range(NT_MAIN)] + [("S", None)]
    for gi in range(n_main_groups):
        all_groups.append(("main", [new_tile(k_, t_) for (k_, t_) in main_tiles[gi * G:(gi + 1) * G]]))

    def issue_loads(grp):
        for tl in grp[1]:
            load_x(tl)

    prep_group(all_groups[0])
    late_setup()

    def do_group(gidx, gkind, tiles):
        for g_, tl_ in enumerate(tiles):
            tl_.g = g_
        if STAGE < 1 or (STAGE < 2 and gkind == "main"):
            return
        nxt = all_groups[gidx + 1] if gidx + 1 < len(all_groups) else None
        if gkind == "pre":
            slabs = (1, 2, 5, 6)
        else:
            slabs = tuple(range(7))
        prev = []
        for slab_i, c in enumerate(slabs):
            wv_, wb_ = take_slab("in", c)
            cur = []
            for tl in tiles:
                bk, bkb = nextbank()
                proj_mm(tl, wv_, wb_, bk, bkb, tl.hbuf, tl.hbufb)
                cur.append((tl, bk, bkb))
            prefetch_slab()
            if gkind == "pre" and nxt is not None and slab_i < len(nxt[1]):
                prep_tile(nxt[1][slab_i])
            for tl in tiles:
                run_post(tl)
            if gkind == "main" and c == 4:
                pass
            for (tl, bk, bkb) in cur:
                evac_slab(tl, c, bk, bkb, full=(gkind == "main"))
            if gkind == "main" and c == 3:
                for tl in tiles:
                    run_post(tl)
                if STAGE < 2.2:
                    return
                for tl in tiles:
                    if tl.kind == "S" and STAGE < 2.5:
                        continue
                    retention(tl)
                if STAGE < 2.7:
                    return
        for tl in tiles:
            run_post(tl)
        if gkind == "pre":
            for tl in tiles:
                state_update(kd[tl.oslot], kdb[tl.oslot], vv[tl.oslot], vvb[tl.oslot], 128, Sst, Sstb, Sbf, Sbfb,
                             None, csb["cdec"])
            return
        if STAGE < 3:
            return
        for tl in tiles:
            load_x(tl)
        mg_ = gidx - n_pre_groups
        bs_ = [mg_] if mg_ < 4 else []
        if mg_ == n_main_groups - 1:
            bs_ += [b_ for b_ in range(4) if b_ >= n_main_groups]
        for b_ in bs_:
            T.dma("pool", lambda e, b_=b_: [e.dma_start(out=wks_d[b_, 0:2040, :], in_=ck_d[b_, 8:2048, :]),
                                          e.dma_start(out=wvs_d[b_, 0:2040, :], in_=cv_d[b_, 8:2048, :])],
                  S_out, n=2, is_output=True)
        s_tl = [tl for tl in tiles if tl.kind == "S"]
        gens_P = [attention_P(tl) for tl in tiles if tl.kind == "P"]
        gen_S = attention_S(s_tl[0]) if s_tl else None
        s_done = gen_S is None
        for gp in gens_P:
            for _ in gp:
                for _k in range(2):
                    if not s_done:
                        if next(gen_S, "done") == "done":
                            s_done = True
        while not s_done:
            if next(gen_S, "done") == "done":
                s_done = True
        if s_tl:
            attn_finish(NS, 4)
            mix_transpose(s_tl[0])
        if STAGE < 4:
            return
        for cbk in range(2):
            wv_, wb_ = take_slab("out", cbk)
            for tl in tiles:
                bk, bkb = nextbank()
                proj_mm(tl, wv_, wb_, bk, bkb, hT[tl.oslot], hTb[tl.oslot])
                xs = tl.xslot
                dve(lambda e, tl=tl, bk=bk, cbk=cbk, xs=xs: e.tensor_tensor(
                    out=xs_[xs][:tl.nr, cbk * 512:(cbk + 1) * 512], in0=bk[:tl.nr, :],
                    in1=xs_[xs][:tl.nr, cbk * 512:(cbk + 1) * 512], op=ALU.add),
                    reads=[bkb, xb[xs]], writes=[xb[xs]])
            prefetch_slab()
        col = 0
        for g, tl in enumerate(tiles):
            tl.col = col
            rmsnorm_T(tl, xs_[tl.xslot], [xb[tl.xslot]], nw2,
                      lambda nr, c0=col: h2T[:, :, c0:c0 + nr], h2Tb[g])
            col += tl.nr
        N = col
        has_S = tiles[-1].kind == "S"
        lastP = [tl for tl in tiles if tl.kind == "P"][-1]
        lp_end = lastP.col + 128
        if STAGE < 5:
            return
        def up_part1(c, wU, wUb, kc):
            p2 = c % 2

            def fu(e, off, bank):
                last = None
                for k in range(8):
                    last = e.matmul(bank[:, :N], lhsT=wU[:, off + k, kc * 128:(kc + 1) * 128], rhs=h2T[:, k, :N],
                                    start=(k == 0), stop=(k == 7))
                return last
            pe(lambda e: fu(e, 0, Bk[p2]), reads=[wUb] + h2Tb, writes=[Bkb[p2]])
            pe(lambda e: fu(e, 8, Bk[2 + p2]), reads=[wUb] + h2Tb, writes=[Bkb[2 + p2]])
            ub_ = uab[p2]
            act(lambda e: e.copy(out=ub_[:, 2:2 + N], in_=Bk[p2][:, :N]), reads=[Bkb[p2]], writes=[uabb[p2]])
            pool(lambda e: e.tensor_copy(out=ub_[:, 0:2], in_=carry[:, c, :]), reads=[carryb], writes=[uabb[p2]])
            if has_S:
                so = tiles[-1].col
                pool(lambda e: e.tensor_copy(
                    out=ub_[:, 2 + so:2 + so + NS].rearrange("p (b t) -> p b t", b=4)[:, :, 0:2],
                    in_=sconvT[:, :, c].rearrange("p (b j) -> p b j", b=4)),
                    reads=[sconvTb, uabb[p2]], writes=[uabb[p2]])
            dve(lambda e: e.tensor_scalar(
                out=cvt[p2][:, :N], in0=ub_[:, 0:N], scalar1=convfm[:, c, 0:1], scalar2=convfm[:, c, 3:4],
                op0=ALU.mult, op1=ALU.add), reads=[uabb[p2], csbb], writes=[cvtb[p2]])
            for jj in (1, 2):
                dve(lambda e, jj=jj: e.scalar_tensor_tensor(
                    out=cvt[p2][:, :N], in0=ub_[:, jj:jj + N], scalar=convfm[:, c, jj:jj + 1],
                    in1=cvt[p2][:, :N], op0=ALU.mult, op1=ALU.add), reads=[uabb[p2], csbb, cvtb[p2]],
                    writes=[cvtb[p2]])

        def up_part2(c):
            p2 = c % 2
            ub_ = uab[p2]
            act(lambda e: e.activation(out=cvt[p2][:, :N], in_=cvt[p2][:, :N], func=AF.Silu),
                reads=[cvtb[p2]], writes=[cvtb[p2]])
            dve(lambda e: e.tensor_tensor(out=gT[:, c, :N], in0=Bk[2 + p2][:, :N], in1=cvt[p2][:, :N], op=ALU.mult),
                reads=[Bkb[2 + p2], cvtb[p2]], writes=[gTb[c]])
            pool(lambda e: e.tensor_copy(out=carry[:, c, :], in_=ub_[:, lp_end:lp_end + 2]),
                 reads=[uabb[p2]], writes=[carryb])
            if lastP.T == NT - 1:
                pool(lambda e: e.tensor_copy(out=convp_st[:, :, c], in_=ub_[:, lp_end:lp_end + 2]),
                     reads=[uabb[p2]], writes=[convp_stb])
            if has_S:
                so = tiles[-1].col
                pool(lambda e: e.tensor_copy(
                    out=convs_st[:, :, c].rearrange("p (b j) -> p b j", b=4),
                    in_=ub_[:, 2 + so:2 + so + NS].rearrange("p (b t) -> p b t", b=4)[:, :, 8:10]),
                    reads=[uabb[p2]], writes=[convs_stb])

        for u_ in range(11):
            if nxt is not None:
                k_, ph_ = divmod(u_, 3)
                if ph_ == 0 and 1 <= k_ <= len(nxt[1]):
                    prep_norm(nxt[1][k_ - 1], "b")
                if k_ < len(nxt[1]):
                    if ph_ == 0:
                        load_x_stage(nxt[1][k_])
                    elif ph_ == 1:
                        prep_norm(nxt[1][k_], "a")
            wU, wUb = take_slab("up", u_)
            for kc in range(2):
                c = u_ * 2 + kc
                up_part1(c, wU, wUb, kc)
                if c >= 1:
                    up_part2(c - 1)
            prefetch_slab()
        up_part2(NCH - 1)
        if STAGE < 6:
            return
        for s_ in range(6):
            wD, wDb = take_slab("dn", s_)
            nchunks = wD.shape[1]
            for g, tl in enumerate(tiles):
                for cbk in range(2):
                    bi = g * 2 + cbk

                    def fd(e, tl=tl, cbk=cbk, bi=bi, wD=wD, nchunks=nchunks, s_=s_):
                        last = None
                        for kc in range(nchunks):
                            c = s_ * 4 + kc
                            last = e.matmul(Bk[bi][:tl.nr, :], lhsT=gT[:, c, tl.col:tl.col + tl.nr],
                                            rhs=wD[:, kc, cbk * 512:(cbk + 1) * 512], start=(c == 0),
                                            stop=(c == NCH - 1))
                        return last
                    pe(fd, reads=[wDb] + gTb[s_ * 4:s_ * 4 + nchunks], writes=[Bkb[bi]])
            prefetch_slab()
        for g, tl in enumerate(tiles):
            xs, nr = tl.xslot, tl.nr
            for cbk in range(2):
                bi = g * 2 + cbk
                dve(lambda e, xs=xs, nr=nr, cbk=cbk, bi=bi: e.tensor_tensor(
                    out=xs_[xs][:nr, cbk * 512:(cbk + 1) * 512], in0=Bk[bi][:nr, :],
                    in1=xs_[xs][:nr, cbk * 512:(cbk + 1) * 512], op=ALU.add),
                    reads=[Bkb[bi], xb[xs]], writes=[xb[xs]])
            act(lambda e, xs=xs, nr=nr: e.activation(out=xn[:nr, :], in_=xs_[xs][:nr, :], func=AF.Square,
                                                     accum_out=st[:nr, 0:1]), reads=[xb[xs]], writes=[xnb, stb])
            act(lambda e, nr=nr: e.activation(out=st[:nr, 1:2], in_=st[:nr, 0:1], func=AF.Sqrt, scale=1.0 / D,
                                              bias=epsT[:nr, :]), reads=[stb, csbb], writes=[stb])
            dve(lambda e, nr=nr: e.reciprocal(out=st[:nr, 2:3], in_=st[:nr, 1:2]), reads=[stb], writes=[stb])
            dve(lambda e, xs=xs, nr=nr: e.scalar_tensor_tensor(out=xs_[xs][:nr, :], in0=xs_[xs][:nr, :],
                                                               scalar=st[:nr, 2:3], in1=nwf[:nr, :], op0=ALU.mult,
                                                               op1=ALU.mult), reads=[xb[xs], stb, csbb],
                writes=[xb[xs]])
            if tl.kind == "S":
                T.dma("pool", lambda e, xs=xs: e.dma_start(out=ys_d[:, :], in_=xs_[xs][:NS, :]), S_y[xs],
                      reads=[xb[xs]], is_output=True)
            else:
                m = tl.T - NT_PRE
                T.dma("pool", lambda e, xs=xs, m=m: e.dma_start(out=y_d[m * 128:(m + 1) * 128, :], in_=xs_[xs][:, :]),
                      S_y[xs], reads=[xb[xs]], is_output=True)
    for gidx_, (gkind_, tiles_) in enumerate(all_groups):
        do_group(gidx_, gkind_, tiles_)

    if STAGE < 7:
        T.final_waits("pool")
        return _materialize(nc, T, es)
    T.dma("pool", lambda e: [e.dma_start(out=ret_d, in_=Sst[:, :, :]), e.dma_start(out=rets_d, in_=Ss[:, :, :, :])],
          S_fin, reads=[Sstb, Ssb], n=2, is_output=True)
    for (src, srcb, nj, dst) in ((convp_st, convp_stb, 2, convp_d), (convs_st, convs_stb, 8, convs_d)):
        nfl = nj * NCH
        srcf = src[:, :, :].rearrange("p j c -> p (j c)")
        dflat = dst.rearrange("j (c f) -> (j c) f", f=128)
        bk, bkb = nextbank()

        def f(e, srcf=srcf, nfl=nfl, bk=bk):
            n0 = min(128, nfl)
            last = e.transpose(out=bk[0:n0, 0:128], in_=srcf[:, 0:n0], identity=csb["ident_f"][:, :])
            if nfl > 128:
                last = e.transpose(out=bk[0:nfl - 128, 128:256], in_=srcf[:, 128:nfl], identity=csb["ident_f"][:, :])
            return last
        pe(f, reads=[srcb, csbb], writes=[bkb])
        n0 = min(128, nfl)
        act(lambda e, bk=bk, n0=n0: e.copy(out=so_tok[0:n0, 0, :], in_=bk[0:n0, 0:128]), reads=[bkb],
            writes=[so_tokb])
        if nfl > 128:
            act(lambda e, bk=bk, nfl=nfl: e.copy(out=so_tok[0:nfl - 128, 1, :], in_=bk[0:nfl - 128, 128:256]),
                reads=[bkb], writes=[so_tokb])

        def fo(e, dflat=dflat, nfl=nfl, n0=n0):
            r = [e.dma_start(out=dflat[0:n0, :], in_=so_tok[0:n0, 0, :])]
            if nfl > 128:
                r.append(e.dma_start(out=dflat[128:nfl, :], in_=so_tok[0:nfl - 128, 1, :]))
            return r
        T.dma("pool", fo, S_sotok, reads=[so_tokb], n=(2 if nfl > 128 else 1), is_output=True)
    T.final_waits("pool")

    return _materialize(nc, T, es)


def _materialize(nc, T, es):
    esem = {e_: es.enter_context(nc.semaphore("sem_" + e_)) for e_ in Tracker.ENG}
    dsem = [es.enter_context(nc.semaphore(f"dsem{i}")) for i in range(len(T.dma_count))]

    def semof(sk):
        return esem[sk[1]] if sk[0] == "e" else dsem[sk[1]]

    def run_stream(name, eng):
        for ent in T.streams[name]:
            if ent[0] == "wait":
                eng.wait_ge(semof(ent[1]), ent[2])
            else:
                _, emit, sk, inc = ent
                r = emit(eng)
                if isinstance(r, (list, tuple)):
                    for ins in r:
                        ins.then_inc(semof(sk), inc)
                else:
                    r.then_inc(semof(sk), inc)

    with nc.Block() as block:
        @block.tensor
        def _(e):
            run_stream("pe", e)

        @block.scalar
        def _(e):
            run_stream("act", e)

        @block.vector
        def _(e):
            run_stream("dve", e)

        @block.gpsimd
        def _(e):
            run_stream("pool", e)

        @block.sync
        def _(e):
            run_stream("sp", e)
    es.close()
    return nc


def _common_inputs(norm1_w, w_in, ret_gn_w, ret_gn_b, w_out, norm2_w, w_up, conv_w, conv_b, w_down, normf_w, NT):
    f = np.float32
    m = {
        "w_in": np.ascontiguousarray(w_in, f), "w_out": np.ascontiguousarray(w_out, f),
        "w_up": np.ascontiguousarray(w_up, f), "w_down": np.ascontiguousarray(w_down, f),
        "gnw_t": np.ascontiguousarray(np.broadcast_to(np.asarray(ret_gn_w, f)[None, :], (128, 512))),
        "gnb_t": np.ascontiguousarray(np.broadcast_to(np.asarray(ret_gn_b, f)[None, :], (128, 512))),
        "nwf_t": np.ascontiguousarray(np.broadcast_to(np.asarray(normf_w, f)[None, :], (128, D))),
        "nw1fm": np.ascontiguousarray(np.asarray(norm1_w, f).reshape(8, 128).T),
        "nw2fm": np.ascontiguousarray(np.asarray(norm2_w, f).reshape(8, 128).T),
    }
    cf = np.concatenate([np.asarray(conv_w, f), np.asarray(conv_b, f)[None, :]], axis=0)
    m["convfm"] = np.ascontiguousarray(cf.reshape(4, NCH, 128).transpose(2, 1, 0))
    for k, v in _consts(NT).items():
        m["c_" + k] = np.ascontiguousarray(v)
    return m


def _sample_inputs(x_sample, state_ret, cache_win_k, cache_win_v, state_conv, c):
    f = np.float32
    xs = np.zeros((NS, D), f)
    for b in range(4):
        xs[b * 10 + 2:b * 10 + 10] = x_sample[4 * c + b]
    return {
        "xs": xs,
        "sret": np.ascontiguousarray(state_ret[4 * c:4 * c + 4], f),
        "ck": np.ascontiguousarray(np.asarray(cache_win_k[4 * c:4 * c + 4], f).reshape(4, 2048, 512)),
        "cv": np.ascontiguousarray(np.asarray(cache_win_v[4 * c:4 * c + 4], f).reshape(4, 2048, 512)),
        "sconv": np.ascontiguousarray(np.asarray(state_conv[4 * c:4 * c + 4], f).reshape(8, DFF)),
    }


def _unpack_state(a):
    return np.ascontiguousarray(a.reshape(2, 64, 4, 64).transpose(2, 0, 1, 3).reshape(8, 64, 64))


_PROG_CACHE = {}


def kernel(x_prompt, x_sample, state_ret, cache_win_k, cache_win_v, state_conv, norm1_w, w_in, ret_gn_w,
           ret_gn_b, w_out, norm2_w, w_up, conv_w, conv_b, w_down, normf_w):
    NT_PRE, NT_MAIN = 15, 17
    NT = NT_PRE + NT_MAIN
    x_prompt = np.asarray(x_prompt, np.float32)
    x_sample = np.asarray(x_sample, np.float32)
    key = (NT_PRE, NT_MAIN)
    if key not in _PROG_CACHE:
        _PROG_CACHE[key] = build_program(NT_PRE, NT_MAIN, out_first_kv_tile=16)
    nc = _PROG_CACHE[key]
    common = _common_inputs(norm1_w, w_in, ret_gn_w, ret_gn_b, w_out, norm2_w, w_up, conv_w, conv_b, w_down,
                            normf_w, NT)
    in_maps = []
    for c in range(8):
        b, role = c // 2, c % 2
        x = np.zeros((NT * 128, D), np.float32)
        vf = np.zeros((128, NT), np.float32)
        if role == 0:
            x[16 * 128:] = x_prompt[b, 0:2048]
            vf[:, 16:] = 1.0
        else:
            x[:] = x_prompt[b]
            vf[:] = 1.0
        m = dict(common)
        m["x"] = x
        m["vflag"] = vf.astype(NPBF)
        m.update(_sample_inputs(x_sample, state_ret, cache_win_k, cache_win_v, state_conv, c))
        in_maps.append(m)
    res = run_bass_kernel_spmd(nc, in_maps, core_ids=list(range(8)))
    R = res.results
    y_prompt = np.zeros((4, 4096, D), np.float32)
    y_sample = np.zeros((32, 8, D), np.float32)
    ret_p = np.zeros((4, 8, 64, 64), np.float32)
    ret_s = np.zeros((32, 8, 64, 64), np.float32)
    wk_p = np.zeros((4, 2048, 8, 64), np.float32)
    wv_p = np.zeros((4, 2048, 8, 64), np.float32)
    wk_s = np.zeros((32, 2048, 8, 64), np.float32)
    wv_s = np.zeros((32, 2048, 8, 64), np.float32)
    conv_p = np.zeros((4, 2, DFF), np.float32)
    conv_s = np.zeros((32, 2, DFF), np.float32)
    for c in range(8):
        b, role = c // 2, c % 2
        r = R[c]
        y_prompt[b, role * 2048:(role + 1) * 2048] = r["y"][128:]
        for bb in range(4):
            y_sample[4 * c + bb] = r["ys"][bb * 10 + 2:bb * 10 + 10]
            ret_s[4 * c + bb] = _unpack_state(r["rets"][:, bb])
        wk_s[4 * c:4 * c + 4] = r["wks"].reshape(4, 2048, 8, 64)
        wv_s[4 * c:4 * c + 4] = r["wvs"].reshape(4, 2048, 8, 64)
        conv_s[4 * c:4 * c + 4] = r["convs"].reshape(4, 2, DFF)
        if role == 1:
            ret_p[b] = _unpack_state(r["ret"])
            wk_p[b] = r["wk"].reshape(2048, 8, 64)
            wv_p[b] = r["wv"].reshape(2048, 8, 64)
            conv_p[b] = r["convp"]
    return (y_prompt, y_sample, ret_p, ret_s, wk_p, wv_p, wk_s, wv_s, conv_p, conv_s)
```
